# Optimizing a Trainium2 kernel written in Bass

```python
import math
import jax, jax.numpy as jnp
from jax import lax
import numpy as np

D_MODEL = 1024
BATCH = 8
SEQ = 2048
DEPTH = 1

MEM_LEN = 256
EPS = 1e-6
NEG_INF = -1e30
BIG = 1e30

NSA_HEADS = 16
NSA_GROUPS = 4
NSA_HPG = NSA_HEADS // NSA_GROUPS
NSA_DH = 64
NSA_SCALE = NSA_DH ** -0.5
CMP_BLOCK = 32
CMP_STRIDE = 16
CMP_HIDDEN = 128
SEL_BLOCK = 64
SEL_TOPK = 16
N_LOCAL_FORCED = 2
WINDOW = 512
SEL_QBLOCK = 32
WIN_QBLOCK = 128

ML_HEADS = 4
ML_DH = 128
ML_CHUNK = 64
CONV_WIDTH = 4

MEM_HEADS = 4
MEM_DH = 128
MEM_SCALE = MEM_DH ** -0.5

D_FF = -(-8 * D_MODEL // (3 * 256)) * 256

N_BRANCH = 3
NSA_Q = NSA_HEADS * NSA_DH
NSA_KV = NSA_GROUPS * NSA_DH
ML_W = ML_HEADS * ML_DH
MEM_W = MEM_HEADS * MEM_DH
IN_SPLITS = (NSA_Q, 6 * NSA_KV, 3 * NSA_HEADS, 3 * ML_W, 2 * ML_HEADS, ML_W, MEM_W, N_BRANCH * D_MODEL)
IN_WIDTH = NSA_Q + 6 * NSA_KV + 3 * NSA_HEADS + 3 * ML_W + 2 * ML_HEADS + ML_W + MEM_W + N_BRANCH * D_MODEL

kernel_name = "hybrid_nsa_mlstm_memxattn_block"


def rmsnorm(x, g):
    xf = x.astype(jnp.float32)
    y = xf * lax.rsqrt(jnp.mean(xf * xf, axis=-1, keepdims=True) + EPS)
    return (y * g.astype(jnp.float32)).astype(x.dtype)


def masked_softmax(s, mask):
    s = jnp.where(mask, s.astype(jnp.float32), NEG_INF)
    return jax.nn.softmax(s, axis=-1) * mask


def compress_blocks(kv, pe, w1, w2):
    S = kv.shape[1]
    n_cmp = (S - CMP_BLOCK) // CMP_STRIDE + 1
    idx = jnp.arange(n_cmp)[:, None] * CMP_STRIDE + jnp.arange(CMP_BLOCK)[None, :]
    blocks = kv[:, idx] + pe[None, None, :, None, :]
    hid = jax.nn.silu(jnp.einsum('bnlgd,ldh->bngh', blocks, w1))
    return jnp.einsum('bngh,hd->bngd', hid, w2)


def nsa_compressed(q, k, v, pe_k, w1_k, w2_k, pe_v, w1_v, w2_v):
    S = q.shape[1]
    kc = compress_blocks(k, pe_k, w1_k, w2_k)
    vc = compress_blocks(v, pe_v, w1_v, w2_v)
    n_cmp = kc.shape[1]
    s = jnp.einsum('bsghd,bngd->bghsn', q, kc) * NSA_SCALE
    t = jnp.arange(S)
    end = jnp.arange(n_cmp) * CMP_STRIDE + CMP_BLOCK - 1
    p = masked_softmax(s, end[None, :] <= t[:, None])
    o = jnp.einsum('bghsn,bngd->bsghd', p.astype(vc.dtype), vc)
    return o, p


def cmp_to_sel_map(n_cmp, n_sel):
    c0 = jnp.arange(n_cmp) * CMP_STRIDE
    s0 = jnp.arange(n_sel) * SEL_BLOCK
    ov = jnp.minimum(c0[:, None] + CMP_BLOCK, s0[None, :] + SEL_BLOCK) - jnp.maximum(c0[:, None], s0[None, :])
    return jnp.clip(ov, 0, None).astype(jnp.float32) / CMP_BLOCK


def nsa_select(p_cmp, S):
    n_cmp = p_cmp.shape[-1]
    n_sel = S // SEL_BLOCK
    imp = jnp.einsum('bghsn,nj->bgsj', p_cmp, cmp_to_sel_map(n_cmp, n_sel))
    qblk = jnp.arange(S) // SEL_BLOCK
    j = jnp.arange(n_sel)
    rel = qblk[:, None] - j[None, :]
    causal = rel >= 0
    forced = causal & ((j[None, :] == 0) | (rel < N_LOCAL_FORCED))
    score = jnp.where(forced, BIG, jnp.where(causal, imp, NEG_INF))
    top_s, top_i = lax.top_k(score, min(SEL_TOPK, n_sel))
    return top_i, top_s > 0.5 * NEG_INF


def nsa_selected(q, k, v, top_i, top_ok):
    B, S, G, HPG, dh = q.shape
    n_sel = S // SEL_BLOCK
    nk = top_i.shape[-1]
    nq = S // SEL_QBLOCK
    kb = k.reshape(B, n_sel, SEL_BLOCK, G, dh).transpose(0, 3, 1, 2, 4)
    vb = v.reshape(B, n_sel, SEL_BLOCK, G, dh).transpose(0, 3, 1, 2, 4)
    qb = jnp.moveaxis(q.reshape(B, nq, SEL_QBLOCK, G, HPG, dh), 1, 0)
    ib = jnp.moveaxis(top_i.reshape(B, G, nq, SEL_QBLOCK, nk), 2, 0)
    mb = jnp.moveaxis(top_ok.reshape(B, G, nq, SEL_QBLOCK, nk), 2, 0)
    bi = jnp.arange(B)[:, None, None, None]
    gi = jnp.arange(G)[None, :, None, None]

    def body(args):
        qi, ii, mi, blk = args
        kg = kb[bi, gi, ii]
        vg = vb[bi, gi, ii]
        tpos = blk * SEL_QBLOCK + jnp.arange(SEL_QBLOCK)
        kpos = ii[..., None] * SEL_BLOCK + jnp.arange(SEL_BLOCK)
        mask = mi[..., None] & (kpos <= tpos[None, None, :, None, None])
        s = jnp.einsum('bqghd,bgqnld->bghqnl', qi, kg) * NSA_SCALE
        p = masked_softmax(s.reshape(B, G, HPG, SEL_QBLOCK, nk * SEL_BLOCK),
                           mask.reshape(B, G, 1, SEL_QBLOCK, nk * SEL_BLOCK))
        p = p.reshape(B, G, HPG, SEL_QBLOCK, nk, SEL_BLOCK).astype(vg.dtype)
        return jnp.einsum('bghqnl,bgqnld->bqghd', p, vg)

    out = lax.map(body, (qb, ib, mb, jnp.arange(nq)))
    return jnp.moveaxis(out, 0, 1).reshape(B, S, G, HPG, dh)


def nsa_window(q, k, v):
    B, S, G, HPG, dh = q.shape
    nq = S // WIN_QBLOCK
    span = WIN_QBLOCK + WINDOW
    kp = jnp.pad(k, ((0, 0), (WINDOW, 0), (0, 0), (0, 0)))
    vp = jnp.pad(v, ((0, 0), (WINDOW, 0), (0, 0), (0, 0)))
    qb = jnp.moveaxis(q.reshape(B, nq, WIN_QBLOCK, G, HPG, dh), 1, 0)

    def body(args):
        qi, blk = args
        start = blk * WIN_QBLOCK
        ks = lax.dynamic_slice_in_dim(kp, start, span, axis=1)
        vs = lax.dynamic_slice_in_dim(vp, start, span, axis=1)
        tpos = start + jnp.arange(WIN_QBLOCK)
        kpos = start - WINDOW + jnp.arange(span)
        d = tpos[:, None] - kpos[None, :]
        mask = (d >= 0) & (d < WINDOW) & (kpos[None, :] >= 0)
        s = jnp.einsum('bqghd,bkgd->bghqk', qi, ks) * NSA_SCALE
        p = masked_softmax(s, mask).astype(vs.dtype)
        return jnp.einsum('bghqk,bkgd->bqghd', p, vs)

    out = lax.map(body, (qb, jnp.arange(nq)))
    return jnp.moveaxis(out, 0, 1).reshape(B, S, G, HPG, dh)


def causal_conv(x, w, b):
    C = x.shape[-1]
    y = lax.conv_general_dilated(x, w[:, None, :], window_strides=(1,), padding=[(CONV_WIDTH - 1, 0)],
                                 dimension_numbers=('NWC', 'WIO', 'NWC'), feature_group_count=C)
    return y + b


def mlstm_chunkwise(q, k, v, i_pre, logf):
    B, S, H, dh = q.shape
    L = ML_CHUNK
    nc = S // L
    q = q.astype(jnp.float32)
    k = k.astype(jnp.float32) * (dh ** -0.5)
    v = v.astype(jnp.float32)

    def chunks(a):
        return jnp.moveaxis(a.reshape(B, nc, L, *a.shape[2:]), 1, 0)

    tril = jnp.tril(jnp.ones((L, L), dtype=bool))

    def step(carry, xs):
        C, n, m = carry
        qj, kj, vj, ij, fj = xs
        qh = qj.transpose(0, 2, 1, 3)
        kh = kj.transpose(0, 2, 1, 3)
        vh = vj.transpose(0, 2, 1, 3)
        b = jnp.cumsum(fj, axis=1).transpose(0, 2, 1)
        ih = ij.transpose(0, 2, 1)
        dlog = jnp.where(tril, b[..., :, None] - b[..., None, :] + ih[..., None, :], -jnp.inf)
        inter = b + m[..., None]
        m_t = jnp.maximum(inter, jnp.max(dlog, axis=-1))
        dw = jnp.exp(dlog - m_t[..., None])
        iw = jnp.exp(inter - m_t)
        sqk = jnp.einsum('bhld,bhsd->bhls', qh, kh) * dw
        num = jnp.einsum('bhls,bhsd->bhld', sqk, vh) + iw[..., None] * jnp.einsum('bhed,bhld->bhle', C, qh)
        den = jnp.sum(sqk, axis=-1) + iw * jnp.einsum('bhd,bhld->bhl', n, qh)
        h = num / jnp.maximum(jnp.abs(den), jnp.exp(-m_t))[..., None]
        b_end = b[..., -1]
        wlog = b_end[..., None] - b + ih
        m_new = jnp.maximum(b_end + m, jnp.max(wlog, axis=-1))
        ws = jnp.exp(wlog - m_new[..., None])
        decay = jnp.exp(b_end + m - m_new)
        C_new = decay[..., None, None] * C + jnp.einsum('bhs,bhse,bhsd->bhed', ws, vh, kh)
        n_new = decay[..., None] * n + jnp.einsum('bhs,bhsd->bhd', ws, kh)
        return (C_new, n_new, m_new), h

    init = (jnp.zeros((B, H, dh, dh), jnp.float32), jnp.zeros((B, H, dh), jnp.float32), jnp.zeros((B, H), jnp.float32))
    _, hs = lax.scan(step, init, (chunks(q), chunks(k), chunks(v), chunks(i_pre), chunks(logf)))
    return hs.transpose(1, 0, 3, 2, 4).reshape(B, S, H, dh)


def memory_xattn(q, mem, g, w_kv):
    B, S, _ = q.shape
    M = mem.shape[1]
    kv = rmsnorm(mem, g) @ w_kv
    k = kv[..., :MEM_W].reshape(B, M, MEM_HEADS, MEM_DH)
    v = kv[..., MEM_W:].reshape(B, M, MEM_HEADS, MEM_DH)
    qh = q.reshape(B, S, MEM_HEADS, MEM_DH)
    s = jnp.einsum('bshd,bmhd->bhsm', qh, k).astype(jnp.float32) * MEM_SCALE
    p = jax.nn.softmax(s, axis=-1).astype(v.dtype)
    return jnp.einsum('bhsm,bmhd->bshd', p, v).reshape(B, S, MEM_W)


def setup_inputs(seed: int = 0) -> dict:
    key = jax.random.key(seed)
    ks = jax.random.split(key, 32)
    L = DEPTH

    def nrm(k, shape, scale):
        return jax.random.normal(k, shape, jnp.float32) * scale

    def gain(k, shape):
        return 1.0 + 0.05 * jax.random.normal(k, shape, jnp.float32)

    gate_b = jnp.concatenate([
        0.1 * jax.random.normal(ks[10], (L, ML_HEADS), jnp.float32),
        jnp.linspace(3.0, 6.0, ML_HEADS, dtype=jnp.float32)[None, :] + 0.1 * jax.random.normal(ks[11], (L, ML_HEADS), jnp.float32),
    ], axis=-1)
    return {
        "x": nrm(ks[0], (BATCH, SEQ, D_MODEL), 1.0),
        "mem": nrm(ks[1], (BATCH, MEM_LEN, D_MODEL), 1.0),
        "g_pre_mix": gain(ks[2], (L, D_MODEL)),
        "w_in": nrm(ks[3], (L, D_MODEL, IN_WIDTH), D_MODEL ** -0.5),
        "cmp_pe_k": nrm(ks[4], (L, CMP_BLOCK, NSA_DH), 0.1),
        "cmp_w1_k": nrm(ks[5], (L, CMP_BLOCK, NSA_DH, CMP_HIDDEN), (CMP_BLOCK * NSA_DH) ** -0.5),
        "cmp_w2_k": nrm(ks[6], (L, CMP_HIDDEN, NSA_DH), CMP_HIDDEN ** -0.5),
        "cmp_pe_v": nrm(ks[7], (L, CMP_BLOCK, NSA_DH), 0.1),
        "cmp_w1_v": nrm(ks[8], (L, CMP_BLOCK, NSA_DH, CMP_HIDDEN), (CMP_BLOCK * NSA_DH) ** -0.5),
        "cmp_w2_v": nrm(ks[9], (L, CMP_HIDDEN, NSA_DH), CMP_HIDDEN ** -0.5),
        "ml_conv_w": nrm(ks[12], (L, CONV_WIDTH, 2 * ML_W), CONV_WIDTH ** -0.5),
        "ml_conv_b": nrm(ks[13], (L, 2 * ML_W), 0.02),
        "ml_gate_b": gate_b,
        "ml_head_g": gain(ks[14], (L, ML_W)),
        "g_mem": gain(ks[15], (L, D_MODEL)),
        "w_mem_kv": nrm(ks[16], (L, D_MODEL, 2 * MEM_W), D_MODEL ** -0.5),
        "w_proj_nsa": nrm(ks[17], (L, NSA_Q, D_MODEL), NSA_Q ** -0.5),
        "w_proj_ml": nrm(ks[18], (L, ML_W, D_MODEL), ML_W ** -0.5),
        "w_proj_mem": nrm(ks[19], (L, MEM_W, D_MODEL), MEM_W ** -0.5),
        "w_out": nrm(ks[20], (L, D_MODEL, D_MODEL), D_MODEL ** -0.5),
        "g_post_mix": gain(ks[21], (L, D_MODEL)),
        "g_pre_ffn": gain(ks[22], (L, D_MODEL)),
        "w_ffn_in": nrm(ks[23], (L, D_MODEL, 2 * D_FF), D_MODEL ** -0.5),
        "w_ffn_down": nrm(ks[24], (L, D_FF, D_MODEL), D_FF ** -0.5),
        "g_post_ffn": gain(ks[25], (L, D_MODEL)),
    }


def reference(x, mem, g_pre_mix, w_in, cmp_pe_k, cmp_w1_k, cmp_w2_k, cmp_pe_v, cmp_w1_v, cmp_w2_v,
              ml_conv_w, ml_conv_b, ml_gate_b, ml_head_g, g_mem, w_mem_kv, w_proj_nsa, w_proj_ml,
              w_proj_mem, w_out, g_post_mix, g_pre_ffn, w_ffn_in, w_ffn_down, g_post_ffn):
    B, S, _ = x.shape
    offsets = np.cumsum(IN_SPLITS)[:-1].tolist()
    for l in range(DEPTH):
        h = rmsnorm(x, g_pre_mix[l])
        z = h @ w_in[l]
        q_nsa, kv_nsa, g_nsa, qkv_ml, if_ml, o_ml, q_mem, g_merge = jnp.split(z, offsets, axis=-1)

        q = q_nsa.reshape(B, S, NSA_GROUPS, NSA_HPG, NSA_DH)
        kv = kv_nsa.reshape(B, S, 6, NSA_GROUPS, NSA_DH)
        o_cmp, p_cmp = nsa_compressed(q, kv[:, :, 0], kv[:, :, 1], cmp_pe_k[l], cmp_w1_k[l], cmp_w2_k[l],
                                      cmp_pe_v[l], cmp_w1_v[l], cmp_w2_v[l])
        top_i, top_ok = nsa_select(p_cmp, S)
        o_slc = nsa_selected(q, kv[:, :, 2], kv[:, :, 3], top_i, top_ok)
        o_win = nsa_window(q, kv[:, :, 4], kv[:, :, 5])
        gb = jax.nn.sigmoid(g_nsa.reshape(B, S, NSA_GROUPS, NSA_HPG, 3))
        y_nsa = (gb[..., 0:1] * o_cmp + gb[..., 1:2] * o_slc + gb[..., 2:3] * o_win).reshape(B, S, NSA_Q)

        qk = jax.nn.silu(causal_conv(qkv_ml[..., :2 * ML_W], ml_conv_w[l], ml_conv_b[l]))
        q_m = qk[..., :ML_W].reshape(B, S, ML_HEADS, ML_DH)
        k_m = qk[..., ML_W:].reshape(B, S, ML_HEADS, ML_DH)
        v_m = qkv_ml[..., 2 * ML_W:].reshape(B, S, ML_HEADS, ML_DH)
        gates = if_ml.astype(jnp.float32) + ml_gate_b[l].astype(jnp.float32)
        i_pre = gates[..., :ML_HEADS]
        logf = jax.nn.log_sigmoid(gates[..., ML_HEADS:])
        h_ml = mlstm_chunkwise(q_m, k_m, v_m, i_pre, logf)
        h_ml = h_ml * lax.rsqrt(jnp.mean(h_ml * h_ml, axis=-1, keepdims=True) + EPS)
        h_ml = h_ml.reshape(B, S, ML_W) * ml_head_g[l].astype(jnp.float32)
        y_ml = (jax.nn.sigmoid(o_ml.astype(jnp.float32)) * h_ml).astype(x.dtype)

        y_mem = memory_xattn(q_mem, mem, g_mem[l], w_mem_kv[l])

        gm = jax.nn.sigmoid(g_merge.reshape(B, S, N_BRANCH, D_MODEL))
        y = (gm[:, :, 0] * (y_nsa @ w_proj_nsa[l]) + gm[:, :, 1] * (y_ml @ w_proj_ml[l])
             + gm[:, :, 2] * (y_mem @ w_proj_mem[l]))
        x = x + rmsnorm(y @ w_out[l], g_post_mix[l])

        h = rmsnorm(x, g_pre_ffn[l])
        gu = h @ w_ffn_in[l]
        gate, up = gu[..., :D_FF], gu[..., D_FF:]
        x = x + rmsnorm((jax.nn.silu(gate) * up) @ w_ffn_down[l], g_post_ffn[l])
    return x
```

```python
import numpy as np
from contextlib import ExitStack
import concourse.bass as bass
import concourse.mybir as mybir
from concourse.bass_utils import run_bass_kernel_spmd

F32 = mybir.dt.float32
BF16 = mybir.dt.bfloat16
AF = mybir.ActivationFunctionType
ALU = mybir.AluOpType
AX = mybir.AxisListType

S, D, NT = 2048, 1024, 16
IN_W = 8248
OFF_KV, OFF_GN, OFF_ML, OFF_IF, OFF_O, OFF_QM, OFF_GM = 1024, 2560, 2608, 4144, 4152, 4664, 5176
DFF = 2816
NF = 22
EPS = 1e-6
BIG = 1e30


class Trk:
    __slots__ = ("name", "w", "r", "dsem", "dcnt")

    def __init__(self, name):
        self.name = name
        self.w = {}
        self.r = {}
        self.dsem = None
        self.dcnt = 0


def _upd(d, sem, cnt):
    k = id(sem)
    if k not in d or d[k][1] < cnt:
        d[k] = (sem, cnt)


class Eng:
    def __init__(self, K, e, name):
        self.K = K
        self.e = e
        self.name = name
        self.sem = K.es.enter_context(K.nc.semaphore("s_" + name))
        self.cnt = 0
        self.seen = {}

    def _sync(self, r, w):
        need = {}
        for t in r:
            for s, (h, v) in t.w.items():
                if need.get(s, (None, -1))[1] < v:
                    need[s] = (h, v)
        for t in w:
            for d in (t.w, t.r):
                for s, (h, v) in d.items():
                    if need.get(s, (None, -1))[1] < v:
                        need[s] = (h, v)
        for s, (h, v) in need.items():
            if s == id(self.sem) and self.name in ("pe", "pool", "sp"):
                continue
            if self.seen.get(s, -1) >= v:
                continue
            self.e.wait_ge(h, v)
            self.seen[s] = v

    def _post(self, inst, r, w):
        self.cnt += 1
        inst.then_inc(self.sem, 1)
        for t in r:
            _upd(t.r, self.sem, self.cnt)
        for t in w:
            _upd(t.w, self.sem, self.cnt)

    def __call__(self, fname, *args, r=(), w=(), **kw):
        self._sync(r, w)
        inst = getattr(self.e, fname)(*args, **kw)
        self._post(inst, r, w)
        return inst

    def group(self, calls, r=(), w=()):
        self._sync(r, w)
        inst = None
        for fname, args, kw in calls:
            inst = getattr(self.e, fname)(*args, **kw)
        self._post(inst, r, w)

    def dma(self, out, in_, st, r=(), w=(), **kw):
        self._sync(r, w)
        K = self.K
        if st.dsem is None:
            st.dsem = K.es.enter_context(K.nc.semaphore("d_" + st.name))
        inst = self.e.dma_start(out=out, in_=in_, **kw)
        st.dcnt += 16
        inst.then_inc(st.dsem, 16)
        for t in r:
            _upd(t.r, st.dsem, st.dcnt)
        for t in w:
            _upd(t.w, st.dsem, st.dcnt)


class Kern:
    def __init__(self, nc, es):
        self.nc = nc
        self.es = es
        self.PE = Eng(self, nc.tensor, "pe")
        self.ACT = Eng(self, nc.scalar, "act")
        self.DVE = Eng(self, nc.vector, "dve")
        self.POOL = Eng(self, nc.gpsimd, "pool")
        self.SP = Eng(self, nc.sync, "sp")
        self.n = 0
        self.alltrk = []

    def trk(self, name=None):
        self.n += 1
        t = Trk(name or f"t{self.n}")
        self.alltrk.append(t)
        return t

    def barrier(self):
        engs = [self.PE, self.ACT, self.DVE, self.POOL, self.SP]
        for e in engs:
            e._sync(self.alltrk, self.alltrk)


def bc(ap, axis, shape):
    return ap.unsqueeze(axis).to_broadcast(list(shape))


import os
DBG_BR = int(os.environ['DBG_BR']) if 'DBG_BR' in os.environ else None


PAD_DEFAULT = 0


class _Stop(Exception):
    pass


def build(debug=False, upto=7):
    nc = bass.Bass("TRN2", target_bir_lowering=False)

    def din(name, shape):
        return nc.dram_tensor(name, list(shape), F32, kind="ExternalInput").ap()

    x = din("x", [S, D])
    mem = din("mem", [256, D])
    g_pre_mix = din("g_pre_mix", [1, D])
    w_in = din("w_in", [D, IN_W])
    cmp_pe_k = din("cmp_pe_k", [32, 64])
    cmp_w1_k = din("cmp_w1_k", [32, 64, 128])
    cmp_w2_k = din("cmp_w2_k", [128, 64])
    cmp_pe_v = din("cmp_pe_v", [32, 64])
    cmp_w1_v = din("cmp_w1_v", [32, 64, 128])
    cmp_w2_v = din("cmp_w2_v", [128, 64])
    ml_conv_w = din("ml_conv_w", [4, 1024])
    ml_conv_b = din("ml_conv_b", [1, 1024])
    ml_gate_b = din("ml_gate_b", [1, 8])
    ml_head_g = din("ml_head_g", [1, 512])
    g_mem = din("g_mem", [1, D])
    w_mem_kv = din("w_mem_kv", [D, 1024])
    w_proj_nsa = din("w_proj_nsa", [1024, D])
    w_proj_ml = din("w_proj_ml", [512, D])
    w_proj_mem = din("w_proj_mem", [512, D])
    w_out = din("w_out", [D, D])
    g_post_mix = din("g_post_mix", [1, D])
    g_pre_ffn = din("g_pre_ffn", [1, D])
    w_ffn_in = din("w_ffn_in", [D, 2 * DFF])
    w_ffn_down = din("w_ffn_down", [DFF, D])
    g_post_ffn = din("g_post_ffn", [1, D])
    c_ident = din("c_ident", [128, 128])
    c_tri = din("c_tri", [128, 128])
    c_trin = din("c_trin", [128, 128])
    c_negm = din("c_negm", [128, 128])
    c_cmpmask = din("c_cmpmask", [128, S])
    c_cmap = din("c_cmap", [128, 32])
    c_selm = din("c_selm", [128, 3 * 16 * 32])
    c_E = din("c_E", [32, 16 * 128])
    out = nc.dram_tensor("out", [S, D], F32, kind="ExternalOutput").ap()
    dbg = {}
    if debug:
        for nm, shp in (("d_ynsaT", [128, 8 * S]), ("d_ymlT", [128, 4 * S]), ("d_ymemT", [128, 4 * S]),
                        ("d_yT", [128, 8 * S]), ("d_hT", [128, 8 * S]), ("d_x1", [S, D])):
            dbg[nm] = nc.dram_tensor(nm, shp, F32, kind="ExternalOutput").ap()

    w_in_r = w_in.rearrange("(kc p) n -> p kc n", p=128)

    try:
      with ExitStack() as es:
        K = Kern(nc, es)
        def ddump(name, ap, shape, r):
            if not debug:
                return
            d = nc.dram_tensor(name, list(shape), F32, kind="ExternalOutput").ap()
            POOL.dma(d, ap, trk("dd_" + name), r=r)
        def stop(k):
            if upto == k:
                if debug and 2 < k < 3:
                    POOL.dma(dbg["d_ynsaT"], y_nsaT[:].rearrange("p a b -> p (a b)"), trk("dbgs"), r=[t_ynsa])
                K.barrier()
                raise _Stop()
        PE, ACT, DVE, POOL, SP = K.PE, K.ACT, K.DVE, K.POOL, K.SP
        trk = K.trk

        def sb(scope, name, shape, dt):
            return scope.enter_context(nc.sbuf_tensor(name, list(shape), dt))

        psF = [es.enter_context(nc.psum_tensor(f"psF{i}", [128, 512], F32)) for i in range(6)]
        t_psF = [trk(f"psF{i}") for i in range(6)]
        psB = [es.enter_context(nc.psum_tensor(f"psB{i}", [128, 1024], BF16)) for i in range(2)]
        t_psB = [trk(f"psB{i}") for i in range(2)]
        rot = {"f": 0, "b": 0, "s": 0, "e": 0}

        def nf():
            i = rot["f"] % 6
            rot["f"] += 1
            return psF[i], t_psF[i]

        def ns():
            i = rot["s"] % 3
            rot["s"] += 1
            return psF[i], t_psF[i]

        def nb():
            i = rot["b"] % 2
            rot["b"] += 1
            return psB[i], t_psB[i]

        def evac(out_ap, in_ap, r, w, scale=None, eng=None):
            if eng is None:
                eng = rot["e"] % 2
                rot["e"] += 1
            if eng == 0:
                if scale is None:
                    ACT("copy", out_ap, in_ap, r=r, w=w)
                else:
                    ACT("mul", out_ap, in_ap, scale, r=r, w=w)
            else:
                if scale is None:
                    DVE("tensor_copy", out_ap, in_ap, r=r, w=w)
                else:
                    DVE("tensor_scalar", out_ap, in_ap, scale, None, ALU.mult, r=r, w=w)

        def mm8(p_ap, lhs_fn, rhs_fn, nk, r, w):
            calls = [("matmul", (p_ap, lhs_fn(kc), rhs_fn(kc)), dict(start=(kc == 0), stop=(kc == nk - 1)))
                     for kc in range(nk)]
            PE.group(calls, r=r, w=w)

        t_c = trk("const")
        idb = sb(es, "idb", [128, 128], BF16)
        tri_b = sb(es, "tri_b", [128, 128], BF16)
        trin_b = sb(es, "trin_b", [128, 128], BF16)
        tri_f = sb(es, "tri_f", [128, 128], F32)
        negm = sb(es, "negm", [128, 128], F32)
        POOL.dma(idb[:], c_ident, t_c, w=[t_c])
        POOL.dma(tri_b[:], c_tri, t_c, w=[t_c])
        POOL.dma(trin_b[:], c_trin, t_c, w=[t_c])
        POOL.dma(tri_f[:], c_tri, t_c, w=[t_c])
        idf = sb(es, "idf", [128, 128], F32)
        POOL.dma(idf[:], c_ident, t_c, w=[t_c])
        POOL.dma(negm[:], c_negm, t_c, w=[t_c])

        npad = int(os.environ.get("PAD", PAD_DEFAULT))
        if npad:
            padt = sb(es, "padt", [128, 128], BF16)
            t_pad = trk("pad")
            for _ in range(npad):
                DVE("memset", padt[:], 0.0, w=[t_pad])
            for _ in range(npad):
                pb_, tpb = nb()
                PE("transpose", pb_[:, 0:128], padt[:], idb[:], r=[t_pad, t_c], w=[tpb])
                ACT("copy", padt[:], pb_[:, 0:128], r=[tpb], w=[t_pad])
        yT = sb(es, "yT", [128, 8, S], BF16)
        t_yT = trk("yT")

        def rms_rstd(st, t_st, n):
            DVE("tensor_scalar", st[:, 1:2], st[:, 0:1], 1.0 / n, EPS, ALU.mult, ALU.add, r=[t_st], w=[t_st])
            ACT("activation", st[:, 1:2], st[:, 1:2], AF.Sqrt, r=[t_st], w=[t_st])
            DVE("reciprocal", st[:, 1:2], st[:, 1:2], r=[t_st], w=[t_st])

        def to_T(src_bf, t_src, dst_fn, t_dst, nchunk):
            pb_, tpb = nb()
            calls = [("transpose", (pb_[:, kc * 128:(kc + 1) * 128], src_bf[:, kc * 128:(kc + 1) * 128], idb[:]), {})
                     for kc in range(nchunk)]
            PE.group(calls, r=[t_src, t_c], w=[tpb])
            evac(dst_fn(), pb_[:, 0:nchunk * 128].rearrange("p (a b) -> p a b", a=nchunk), r=[tpb], w=[t_dst])

        with ExitStack() as sB:
            hT = sb(sB, "hT", [128, 8, S], BF16)
            t_hT = [trk(f"hT{i}") for i in range(NT)]
            y_nsaT = sb(sB, "y_nsaT", [128, 8, S], BF16)
            t_ynsa = trk("ynsaT")

            with ExitStack() as ph:
                gb = sb(ph, "gb1", [128, D], F32)
                t_gb = trk("gb1")
                SP.dma(gb[:], g_pre_mix.partition_broadcast(128), t_gb, w=[t_gb])
                xt = [sb(ph, f"xt{i}", [128, D], F32) for i in range(2)]
                t_xt = [trk(f"xt{i}") for i in range(2)]
                sq = sb(ph, "sq1", [128, D], F32)
                t_sq = trk("sq1")
                hn = [sb(ph, f"hn{i}", [128, D], BF16) for i in range(2)]
                t_hn = [trk(f"hn{i}") for i in range(2)]
                st = [sb(ph, f"st{i}", [128, 2], F32) for i in range(2)]
                t_st = [trk(f"st{i}") for i in range(2)]
                for tt in range(NT):
                    i = tt % 2
                    SP.dma(xt[i][:], x[tt * 128:(tt + 1) * 128, :], t_xt[i], w=[t_xt[i]])
                    ACT("activation", sq[:], xt[i][:], AF.Square, accum_out=st[i][:, 0:1], r=[t_xt[i]], w=[t_sq, t_st[i]])
                    rms_rstd(st[i], t_st[i], D)
                    DVE("scalar_tensor_tensor", hn[i][:], xt[i][:], st[i][:, 1:2], gb[:], ALU.mult, ALU.mult,
                        r=[t_xt[i], t_st[i], t_gb], w=[t_hn[i]])
                    to_T(hn[i], t_hn[i], lambda tt=tt: hT[:, :, tt * 128:(tt + 1) * 128], t_hT[tt], 8)
                K.barrier()
            if debug:
                POOL.dma(dbg["d_hT"], hT[:].rearrange("p a b -> p (a b)"), trk("dbg0"), r=t_hT)
            stop(1)

            with ExitStack() as ph:
                cmpmask = sb(ph, "cmpmask", [128, S], BF16)
                selm = sb(ph, "selm", [128, 3, 16, 32], F32)
                Esel = sb(ph, "Esel", [32, 16, 128], BF16)
                POOL.dma(cmpmask[:], c_cmpmask, t_c, w=[t_c])
                POOL.dma(selm[:].rearrange("p a b c -> p (a b c)"), c_selm, t_c, w=[t_c])
                POOL.dma(Esel[:].rearrange("p a b -> p (a b)"), c_E, t_c, w=[t_c])
                w1 = sb(ph, "w1kv", [128, 32, 128], BF16)
                w2k = sb(ph, "w2k", [128, 64], BF16)
                w2v = sb(ph, "w2v", [128, 64], BF16)
                peT = sb(ph, "peT", [128, 32], BF16)
                t_cw = trk("cmpw")
                POOL.dma(w1[0:64], cmp_w1_k.rearrange("l d h -> d l h"), t_cw, w=[t_cw])
                POOL.dma(w1[64:128], cmp_w1_v.rearrange("l d h -> d l h"), t_cw, w=[t_cw])
                POOL.dma(w2k[:], cmp_w2_k, t_cw, w=[t_cw])
                POOL.dma(w2v[:], cmp_w2_v, t_cw, w=[t_cw])
                pe_sb = sb(ph, "pe_sb", [32, 128], F32)
                t_pe = trk("pe_sb")
                SP.dma(pe_sb[:, 0:64], cmp_pe_k, t_pe, w=[t_pe])
                SP.dma(pe_sb[:, 64:128], cmp_pe_v, t_pe, w=[t_pe])
                p, tp = nf()
                PE("transpose", p[:, 0:32], pe_sb[:], idf[0:32, 0:32], r=[t_pe, t_c], w=[tp])
                DVE("tensor_copy", peT[:], p[:, 0:32], r=[tp], w=[t_cw])
                cbias = sb(ph, "cbias", [128, 2], F32)
                t_cb = trk("cbias")
                for j in range(2):
                    p, tp = nf()
                    lo = j * 64
                    calls = [("matmul", (p[:, 0:1], w1[lo:lo + 64, l, :], peT[lo:lo + 64, l:l + 1]),
                              dict(start=(l == 0), stop=(l == 31))) for l in range(32)]
                    PE.group(calls, r=[t_cw], w=[tp])
                    DVE("tensor_copy", cbias[:, j:j + 1], p[:, 0:1], r=[tp], w=[t_cb])
                stop(2.1)

                wg = sb(ph, "wgrp", [128, 8, 652], BF16)
                t_wg = trk("wgrp")
                qTg = sb(ph, "qTg", [96, 4, S], BF16)
                t_q = trk("qTg")
                kvc = sb(ph, "kvc", [128, S], BF16)
                t_kvc = trk("kvc")
                kslcT = sb(ph, "kslcT", [96, S], BF16)
                kwinT = sb(ph, "kwinT", [64, S], BF16)
                t_ks = trk("kslcT")
                t_kw = trk("kwinT")
                vs = sb(ph, "vs", [128, NT, 65], BF16)
                vw = sb(ph, "vw", [128, NT, 65], BF16)
                t_vs = trk("vs")
                t_vw = trk("vw")
                sg = sb(ph, "sg", [128, NT, 12], F32)
                t_sg = trk("sg")
                hid = sb(ph, "hid", [128, 2, 128], BF16)
                t_hid = trk("hid")
                kcT = sb(ph, "kcT", [64, 128], BF16)
                t_kcT = trk("kcT")
                vca = sb(ph, "vca", [128, 98], BF16)
                t_vca = trk("vca")
                PcT = sb(ph, "PcT", [128, 512], BF16)
                t_PcT = trk("PcT")
                Pt = [sb(ph, f"Pt{i}", [128, 512], BF16) for i in range(3)]
                t_Pt = [trk(f"Pt{i}") for i in range(3)]
                rc = sb(ph, "rc", [128, 4], F32)
                t_rc = trk("rc")
                impt = sb(ph, "impt", [128, 4, 32], F32)
                imp = sb(ph, "imp", [128, 32], F32)
                sc = sb(ph, "sc", [128, 32], F32)
                cmpb = sb(ph, "cmpb", [128, 32, 32], BF16)
                rank = sb(ph, "rank", [128, 32], F32)
                sel = sb(ph, "sel", [128, 32], F32)
                biasq = sb(ph, "biasq", [128, 96], BF16)
                t_sel = trk("selwork")
                t_bq = trk("biasq")
                biasT = sb(ph, "biasT", [32, 4, 128], BF16)
                t_bT = trk("biasT")
                fac = sb(ph, "fac", [128, 3, 4], F32)
                t_fac = trk("fac")
                yacc = sb(ph, "yacc", [128, 4, 64], F32)
                ytmp = sb(ph, "ytmp", [128, 4, 64], F32)
                ytmp2 = sb(ph, "ytmp2", [128, 4, 64], F32)
                ybf = sb(ph, "ybf", [128, 256], BF16)
                t_ya = trk("yacc")
                t_ybf = trk("ybf")
                DVE("memset", vs[:, :, 64:65], 1.0, w=[t_vs])
                DVE("memset", vw[:, :, 64:65], 1.0, w=[t_vw])
                DVE("memset", vca[:], 0.0, w=[t_vca])
                DVE("memset", vca[:, 64:65], 1.0, w=[t_vca])
                POOL.dma(vca[:, 66:98], c_cmap, t_c, w=[t_c, t_vca])
                DVE("memset", hid[:], 0.0, w=[t_hid])
                DVE("memset", kcT[:], 0.0, w=[t_kcT])
                DVE("memset", biasq[:], 0.0, w=[t_bq])
                POOL.dma(kslcT[64:96, :], c_E, t_c, w=[t_c, t_ks])

                for g in range(4):
                    POOL.dma(wg[:, :, 0:256], w_in_r[:, :, g * 256:(g + 1) * 256], t_wg, w=[t_wg])
                    for si, slot in enumerate((0, 1, 2, 4, 3, 5)):
                        c0 = OFF_KV + slot * 256 + g * 64
                        POOL.dma(wg[:, :, 256 + si * 64:256 + (si + 1) * 64], w_in_r[:, :, c0:c0 + 64], t_wg, w=[t_wg])
                    POOL.dma(wg[:, :, 640:652], w_in_r[:, :, OFF_GN + g * 12:OFF_GN + (g + 1) * 12], t_wg, w=[t_wg])
                    for tb in range(4):
                        ts_ = slice(tb * 512, (tb + 1) * 512)
                        rr = [t_wg] + t_hT[4 * tb:4 * tb + 4]
                        for hh in range(4):
                            p, tp = nf()
                            mm8(p[0:64, :], lambda kc: wg[:, kc, hh * 64:(hh + 1) * 64], lambda kc: hT[:, kc, ts_], 8, rr, [tp])
                            evac(qTg[0:64, hh, ts_], p[0:64, :], [tp], [t_q], scale=0.125)
                        p, tp = nf()
                        mm8(p[:, :], lambda kc: wg[:, kc, 256:384], lambda kc: hT[:, kc, ts_], 8, rr, [tp])
                        evac(kvc[:, ts_], p[:, :], [tp], [t_kvc])
                        p, tp = nf()
                        mm8(p[0:64, :], lambda kc: wg[:, kc, 384:448], lambda kc: hT[:, kc, ts_], 8, rr, [tp])
                        evac(kslcT[0:64, ts_], p[0:64, :], [tp], [t_ks])
                        p, tp = nf()
                        mm8(p[0:64, :], lambda kc: wg[:, kc, 448:512], lambda kc: hT[:, kc, ts_], 8, rr, [tp])
                        evac(kwinT[:, ts_], p[0:64, :], [tp], [t_kw])
                    for tt in range(NT):
                        tsl = slice(tt * 128, (tt + 1) * 128)
                        p, tp = nf()
                        calls = []
                        for kc in range(8):
                            calls.append(("matmul", (p[:, 0:128], hT[:, kc, tsl], wg[:, kc, 512:640]),
                                          dict(start=(kc == 0), stop=(kc == 7))))
                        for kc in range(8):
                            calls.append(("matmul", (p[:, 128:140], hT[:, kc, tsl], wg[:, kc, 640:652]),
                                          dict(start=(kc == 0), stop=(kc == 7))))
                        PE.group(calls, r=[t_wg, t_hT[tt]], w=[tp])
                        DVE("tensor_copy", vs[:, tt, 0:64], p[:, 0:64], r=[tp], w=[t_vs])
                        DVE("tensor_copy", vw[:, tt, 0:64], p[:, 64:128], r=[tp], w=[t_vw])
                        ACT("activation", sg[:, tt, :], p[:, 128:140], AF.Sigmoid, r=[tp], w=[t_sg])
                    stop(2.2)
                    for j in range(2):
                        lo = j * 64
                        p, tp = nf()
                        calls = [("matmul", (p[:, 0:127], w1[lo:lo + 64, l, :], kvc[lo:lo + 64, l:l + 2017:16]),
                                  dict(start=(l == 0), stop=(l == 31))) for l in range(32)]
                        PE.group(calls, r=[t_cw, t_kvc], w=[tp])
                        ACT("activation", hid[:, j, 0:127], p[:, 0:127], AF.Silu, bias=cbias[:, j:j + 1],
                            r=[tp, t_cb], w=[t_hid])
                    p, tp = nf()
                    PE("matmul", p[0:64, 0:127], w2k[:], hid[:, 0, 0:127], start=True, stop=True, r=[t_cw, t_hid], w=[tp])
                    DVE("tensor_copy", kcT[:, 0:127], p[0:64, 0:127], r=[tp], w=[t_kcT])
                    p, tp = nf()
                    PE("matmul", p[0:127, 0:64], hid[:, 1, 0:127], w2v[:], start=True, stop=True, r=[t_cw, t_hid], w=[tp])
                    DVE("tensor_copy", vca[0:127, 0:64], p[0:127, 0:64], r=[tp], w=[t_vca])

                    if g == 0:
                        ddump("dd_hid", hid[:].rearrange("p a b -> p (a b)"), [128, 256], [t_hid])
                        ddump("dd_kcT", kcT[:], [64, 128], [t_kcT])
                        ddump("dd_vca", vca[:], [128, 98], [t_vca])
                        ddump("dd_cbias", cbias[:], [128, 2], [t_cb])
                        ddump("dd_kvc", kvc[:], [128, S], [t_kvc])
                        ddump("dd_peT", peT[:], [128, 32], [t_cw])
                    stop(2.3)
                    pOc, tOc = psF[3], t_psF[3]
                    pOs, tOs = psF[4], t_psF[4]
                    pOw, tOw = psF[5], t_psF[5]
                    pOc3 = pOc[:].rearrange("p (a b) -> p a b", a=4)
                    pOs3 = pOs[:].rearrange("p (a b) -> p a b", a=4)
                    pOw3 = pOw[:].rearrange("p (a b) -> p a b", a=4)
                    for qt in range(NT):
                        qs = slice(qt * 128, (qt + 1) * 128)
                        q4 = qTg[0:64, :, qs]
                        q4b = qTg[0:96, :, qs]
                        Pc3 = PcT[:].rearrange("p (a b) -> p a b", a=4)
                        k0 = max(0, qt - 4)
                        steps = [("c", 0)] + [("w", kt) for kt in range(k0, qt + 1)] + [("s", kt) for kt in range(qt + 1)]
                        state = {}

                        def emit_S(idx):
                            br, kt = steps[idx]
                            ks = slice(kt * 128, (kt + 1) * 128)
                            pS, tS = ns()
                            pS3 = pS[:].rearrange("p (a b) -> p a b", a=4)
                            if br == "c":
                                PE("matmul", pS3[0:127], kcT[:, 0:127], q4, start=True, stop=True, r=[t_kcT, t_q], w=[tS])
                                ACT("activation", PcT[0:127, :], pS[0:127, :], AF.Exp, r=[tS], w=[t_PcT])
                                DVE("tensor_tensor", Pc3[0:127], Pc3[0:127], bc(cmpmask[0:127, qs], 1, [127, 4, 128]), ALU.mult,
                                    r=[t_c], w=[t_PcT])
                                return
                            if br == "w":
                                PE("matmul", pS3, kwinT[:, ks], q4, start=True, stop=True, r=[t_kw, t_q], w=[tS])
                            else:
                                if qt < 8:
                                    PE("matmul", pS3, kslcT[0:64, ks], q4, start=True, stop=True, r=[t_ks, t_q], w=[tS])
                                else:
                                    PE("matmul", pS3, kslcT[0:96, ks], q4b, start=True, stop=True, r=[t_ks, t_q, t_c], w=[tS])
                            i = rot.setdefault("p", 0) % 3
                            rot["p"] += 1
                            state[idx] = i
                            ACT("activation", Pt[i][:], pS[:], AF.Exp, r=[tS], w=[t_Pt[i]])
                            P3 = Pt[i][:].rearrange("p (a b) -> p a b", a=4)
                            if kt == qt:
                                DVE("tensor_tensor", P3, P3, bc(tri_b[:], 1, [128, 4, 128]), ALU.mult, r=[t_c], w=[t_Pt[i]])
                            elif br == "w" and kt == qt - 4:
                                DVE("tensor_tensor", P3, P3, bc(trin_b[:], 1, [128, 4, 128]), ALU.mult, r=[t_c], w=[t_Pt[i]])

                        def emit_PV(idx):
                            br, kt = steps[idx]
                            if br == "c":
                                calls = [("matmul", (pOc3[:, hh, 0:98], PcT[0:127, hh * 128:(hh + 1) * 128], vca[0:127, 0:98]),
                                          dict(start=True, stop=True)) for hh in range(4)]
                                PE.group(calls, r=[t_PcT, t_vca], w=[tOc])
                                DVE("tensor_scalar_max", rc[:, 0:4], pOc3[:, :, 64], 1e-20, r=[tOc], w=[t_rc])
                                DVE("reciprocal", rc[:, 0:4], rc[:, 0:4], r=[t_rc], w=[t_rc])
                                if qt < 8:
                                    DVE("tensor_scalar", biasq[:, 64:96], selm[:, 2, qt, :], -1.0, 30000.0, ALU.add, ALU.mult,
                                        r=[t_c], w=[t_bq])
                                    return
                                DVE("tensor_tensor", impt[:], pOc3[:, :, 66:98], bc(rc[:, 0:4], 2, [128, 4, 32]), ALU.mult,
                                    r=[tOc, t_rc], w=[t_sel])
                                DVE("tensor_reduce", imp[:], impt[:].rearrange("p h j -> p j h"), AX.X, ALU.add, r=[t_sel], w=[t_sel])
                                DVE("tensor_tensor", sc[:], imp[:], selm[:, 0, qt, :], ALU.mult, r=[t_sel, t_c], w=[t_sel])
                                DVE("tensor_tensor", sc[:], sc[:], selm[:, 1, qt, :], ALU.add, r=[t_sel, t_c], w=[t_sel])
                                DVE("tensor_tensor", cmpb[:], bc(sc[:], 1, [128, 32, 32]), bc(sc[:], 2, [128, 32, 32]), ALU.is_gt,
                                    r=[t_sel], w=[t_sel])
                                DVE("tensor_reduce", rank[:], cmpb[:], AX.X, ALU.add, r=[t_sel], w=[t_sel])
                                DVE("scalar_tensor_tensor", sel[:], rank[:], 15.5, selm[:, 2, qt, :], ALU.is_lt, ALU.mult,
                                    r=[t_sel, t_c], w=[t_sel])
                                DVE("tensor_scalar", biasq[:, 64:96], sel[:], -1.0, 30000.0, ALU.add, ALU.mult, r=[t_sel], w=[t_bq])
                                return
                            i = state[idx]
                            if br == "w":
                                calls = [("matmul", (pOw3[:, hh, 0:65], Pt[i][:, hh * 128:(hh + 1) * 128], vw[:, kt, :]),
                                          dict(start=(kt == k0 and hh == 0), stop=(kt == qt))) for hh in range(4)]
                                PE.group(calls, r=[t_Pt[i], t_vw], w=[tOw])
                            else:
                                calls = [("matmul", (pOs3[:, hh, 0:65], Pt[i][:, hh * 128:(hh + 1) * 128], vs[:, kt, :]),
                                          dict(start=(kt == 0 and hh == 0), stop=(kt == qt))) for hh in range(4)]
                                PE.group(calls, r=[t_Pt[i], t_vs], w=[tOs])

                        SK = 2
                        nrem = len(steps) - 1
                        emit_S(0)
                        for j in range(nrem + SK):
                            if j < nrem:
                                if steps[1 + j] == ("s", 0):
                                    if qt >= 8:
                                        pb_, tpb = nb()
                                        PE("transpose", pb_[0:96, 0:128], biasq[:], idb[:], r=[t_bq, t_c], w=[tpb])
                                        DVE("tensor_copy", qTg[64:96, :, qs], bc(pb_[64:96, 0:128], 1, [32, 4, 128]),
                                            r=[tpb], w=[t_q])
                                emit_S(1 + j)
                            if j == 0:
                                emit_PV(0)
                            if j >= SK:
                                emit_PV(1 + j - SK)
                        DVE("tensor_copy", fac[:, 0, :], rc[:, 0:4], r=[t_rc], w=[t_fac])
                        DVE("reciprocal", fac[:, 1, :], pOs3[:, :, 64], r=[tOs], w=[t_fac])
                        DVE("reciprocal", fac[:, 2, :], pOw3[:, :, 64], r=[tOw], w=[t_fac])
                        if DBG_BR is None:
                            DVE("tensor_tensor", fac[:], fac[:], sg[:, qt, :].rearrange("p (h b) -> p b h", b=3), ALU.mult,
                                r=[t_sg], w=[t_fac])
                        else:
                            for bb in range(3):
                                if bb != DBG_BR:
                                    DVE("memset", fac[:, bb, :], 0.0, w=[t_fac])
                        DVE("tensor_tensor", yacc[:], pOc3[:, :, 0:64], bc(fac[:, 0, :], 2, [128, 4, 64]), ALU.mult,
                            r=[tOc, t_fac], w=[t_ya])
                        DVE("tensor_tensor", ytmp[:], pOs3[:, :, 0:64], bc(fac[:, 1, :], 2, [128, 4, 64]), ALU.mult,
                            r=[tOs, t_fac], w=[t_ya])
                        DVE("tensor_tensor", ytmp2[:], pOw3[:, :, 0:64], bc(fac[:, 2, :], 2, [128, 4, 64]), ALU.mult,
                            r=[tOw, t_fac], w=[t_ya])
                        DVE("tensor_tensor", yacc[:], yacc[:], ytmp[:], ALU.add, r=[t_ya], w=[t_ya])
                        DVE("tensor_tensor", ybf[:].rearrange("p (a b) -> p a b", a=4), yacc[:], ytmp2[:], ALU.add,
                            r=[t_ya], w=[t_ybf])
                        to_T(ybf, t_ybf, lambda g=g, qs=qs: y_nsaT[:, 2 * g:2 * g + 2, qs], t_ynsa, 2)
                        if qt == 1 and g == 0 and debug:
                            dtmp = sb(ph, "dtmp", [128, 512], F32)
                            DVE("tensor_copy", dtmp[:], pOc[:], r=[tOc], w=[t_ya])
                            ddump("dd_pOc", dtmp[:], [128, 512], [t_ya])
                            ddump("dd_PcT", PcT[:], [128, 512], [t_PcT])
                            ddump("dd_rc", rc[:], [128, 4], [t_rc])
                            ddump("dd_fac", fac[:].rearrange("p a b -> p (a b)"), [128, 12], [t_fac])
                            ddump("dd_yacc", yacc[:].rearrange("p a b -> p (a b)"), [128, 256], [t_ya])
                            ddump("dd_q4", qTg[:, :, qs], [64, 4, 128], [t_q])
                            stop(2.45)
                        if qt == 0:
                            stop(2.4)
                        if qt == 5:
                            stop(2.5)
                K.barrier()
            if debug:
                POOL.dma(dbg["d_ynsaT"], y_nsaT[:].rearrange("p a b -> p (a b)"), trk("dbg1"), r=[t_ynsa])
            stop(2)

            y_mlT = sb(sB, "y_mlT", [128, 4, S], BF16)
            t_yml = trk("ymlT")
            with ExitStack() as ph:
                wif = sb(ph, "wif", [128, 8, 8], BF16)
                t_wif = trk("wif")
                POOL.dma(wif[:], w_in_r[:, :, OFF_IF:OFF_IF + 8], t_wif, w=[t_wif])
                gbias = sb(ph, "gbias", [128, 8], F32)
                hgb = sb(ph, "hgb", [128, 512], F32)
                cwT = sb(ph, "cwT", [128, 4, 8], F32)
                cbT = sb(ph, "cbT", [128, 8], F32)
                t_mc = trk("mlconst")
                SP.dma(gbias[:], ml_gate_b.partition_broadcast(128), t_mc, w=[t_mc])
                SP.dma(hgb[:], ml_head_g.partition_broadcast(128), t_mc, w=[t_mc])
                cw_sb = sb(ph, "cw_sb", [5, 1024], F32)
                SP.dma(cw_sb[0:4, :], ml_conv_w, t_mc, w=[t_mc])
                SP.dma(cw_sb[4:5, :], ml_conv_b, t_mc, w=[t_mc])
                p, tp = nf()
                PE.group([("transpose", (p[:, j * 5:(j + 1) * 5], cw_sb[:, j * 128:(j + 1) * 128], idf[0:5, 0:5]), {})
                          for j in range(8)], r=[t_mc, t_c], w=[tp])
                p3 = p[:, 0:40].rearrange("p (j w) -> p j w", w=5)
                DVE("tensor_copy", cwT[:], p3[:, :, 0:4].rearrange("p j w -> p w j"), r=[tp], w=[t_mc])
                DVE("tensor_copy", cbT[:], p3[:, :, 4], r=[tp], w=[t_mc])
                stop(3.1)
                ifz = sb(ph, "ifz", [128, NT, 8], F32)
                t_ifz = trk("ifz")
                logf = sb(ph, "logf", [128, NT, 4], F32)
                t_lf = trk("logf")
                ib = sb(ph, "ib", [128, NT, 4], F32)
                t_ib = trk("ib")
                for tt in range(NT):
                    tsl = slice(tt * 128, (tt + 1) * 128)
                    p, tp = nf()
                    mm8(p[:, 0:8], lambda kc: hT[:, kc, tsl], lambda kc: wif[:, kc, :], 8, [t_wif, t_hT[tt]], [tp])
                    DVE("tensor_tensor", ifz[:, tt, :], p[:, 0:8], gbias[:], ALU.add, r=[tp, t_mc], w=[t_ifz])
                ACT("activation", logf[:], ifz[:, :, 4:8], AF.Exp, scale=-1.0, r=[t_ifz], w=[t_lf])
                DVE("tensor_scalar_add", logf[:], logf[:], 1.0, r=[t_lf], w=[t_lf])
                ACT("activation", logf[:], logf[:], AF.Ln, r=[t_lf], w=[t_lf])
                DVE("tensor_scalar", logf[:], logf[:], -1.0, None, ALU.mult, r=[t_lf], w=[t_lf])
                lf_hi = sb(ph, "lf_hi", [128, NT, 4], BF16)
                lf_lo = sb(ph, "lf_lo", [128, NT, 4], BF16)
                lf_t = sb(ph, "lf_t", [128, NT, 4], F32)
                DVE("tensor_copy", lf_hi[:], logf[:], r=[t_lf], w=[t_lf])
                DVE("tensor_copy", lf_t[:], lf_hi[:], r=[t_lf], w=[t_lf])
                DVE("tensor_tensor", lf_t[:], logf[:], lf_t[:], ALU.subtract, r=[t_lf], w=[t_lf])
                DVE("tensor_copy", lf_lo[:], lf_t[:], r=[t_lf], w=[t_lf])
                for tt in range(NT):
                    p, tp = nf()
                    PE.group([("matmul", (p[:, 0:4], tri_b[:], lf_hi[:, tt, :]), dict(start=True, stop=False)),
                              ("matmul", (p[:, 0:4], tri_b[:], lf_lo[:, tt, :]), dict(start=False, stop=True))],
                             r=[t_c, t_lf], w=[tp])
                    DVE("tensor_tensor", ib[:, tt, :], ifz[:, tt, 0:4], p[:, 0:4], ALU.subtract, r=[tp, t_ifz], w=[t_ib])

                stop(3.2)
                wml = [sb(ph, f"wml{i}", [128, 8, 512], BF16) for i in range(2)]
                t_wml = [trk(f"wml{i}") for i in range(2)]
                pre = sb(ph, "pre", [128, S + 3], F32)
                t_pre = trk("pre")
                cacc = sb(ph, "cacc", [128, S], F32)
                t_cacc = trk("cacc")
                qm = sb(ph, "qm", [128, S], BF16)
                km = sb(ph, "km", [128, S], BF16)
                t_qm = trk("qm")
                t_km = trk("km")
                ktok = sb(ph, "ktok", [128, NT, 128], BF16)
                t_kt = trk("ktok")
                va = sb(ph, "va", [128, NT, 129], BF16)
                t_va = trk("va")
                vT = sb(ph, "vT", [128, S], BF16)
                t_vT = trk("vT")
                ogT = sb(ph, "ogT", [128, S], BF16)
                t_ogT = trk("ogT")
                Ct = sb(ph, "Ct", [128, 129], F32)
                Ctb = sb(ph, "Ctb", [128, 129], BF16)
                t_Ct = trk("Ct")
                t_Ctb = trk("Ctb")
                lfb = [sb(ph, f"lfb{i}", [128, 2, 128], BF16) for i in range(2)]
                t_lfb = [trk(f"lfb{i}") for i in range(2)]
                arg = [sb(ph, f"arg{i}", [128, 128], F32) for i in range(2)]
                DT = [sb(ph, f"DT{i}", [128, 128], F32) for i in range(2)]
                t_DT = [trk(f"DT{i}") for i in range(2)]
                eb = [sb(ph, f"eb{i}", [128, 128], F32) for i in range(2)]
                t_eb = [trk(f"eb{i}") for i in range(2)]
                sm = [sb(ph, f"sm{i}", [128, 8], F32) for i in range(2)]
                t_sm = [trk(f"sm{i}") for i in range(2)]
                sqk = [sb(ph, f"sqk{i}", [128, 128], BF16) for i in range(2)]
                t_sqk = [trk(f"sqk{i}") for i in range(2)]
                qsc = [sb(ph, f"qsc{i}", [128, 128], BF16) for i in range(2)]
                t_qsc = [trk(f"qsc{i}") for i in range(2)]
                vwt = [sb(ph, f"vwt{i}", [128, 129], BF16) for i in range(2)]
                t_vwt = [trk(f"vwt{i}") for i in range(2)]
                hnm = sb(ph, "hnm", [128, 128], F32)
                hsq = sb(ph, "hsq", [128, 128], F32)
                hst = sb(ph, "hst", [128, 4], F32)
                t_h = trk("hwork")
                yml = sb(ph, "yml", [128, 128], BF16)
                t_ymlb = trk("ymlb")
                DVE("memset", pre[:, 0:3], 0.0, w=[t_pre])
                DVE("memset", va[:, :, 128:129], 1.0, w=[t_va])
                for hd in range(4):
                    wi = hd % 2
                    for pi, c0 in enumerate((OFF_ML + hd * 128, OFF_ML + 512 + hd * 128, OFF_ML + 1024 + hd * 128,
                                             OFF_O + hd * 128)):
                        POOL.dma(wml[wi][:, :, pi * 128:(pi + 1) * 128], w_in_r[:, :, c0:c0 + 128], t_wml[wi], w=[t_wml[wi]])
                    for qk in range(2):
                        j = qk * 4 + hd
                        for tb in range(4):
                            ts_ = slice(tb * 512, (tb + 1) * 512)
                            p, tp = nf()
                            mm8(p[:, :], lambda kc: wml[wi][:, kc, qk * 128:(qk + 1) * 128], lambda kc: hT[:, kc, ts_], 8,
                                [t_wml[wi]] + t_hT[4 * tb:4 * tb + 4], [tp])
                            evac(pre[:, 3 + tb * 512:3 + (tb + 1) * 512], p[:, :], [tp], [t_pre])
                        DVE("tensor_scalar", cacc[:], pre[:, 3:3 + S], cwT[:, 3, j:j + 1], None, ALU.mult,
                            r=[t_pre, t_mc], w=[t_cacc])
                        for wv in range(3):
                            DVE("scalar_tensor_tensor", cacc[:], pre[:, wv:wv + S], cwT[:, wv, j:j + 1], cacc[:],
                                ALU.mult, ALU.add, r=[t_pre, t_mc], w=[t_cacc])
                        dst, tdst = (qm, t_qm) if qk == 0 else (km, t_km)
                        ACT("activation", dst[:], cacc[:], AF.Silu, bias=cbT[:, j:j + 1], r=[t_cacc, t_mc], w=[tdst])
                    stop(3.3)
                    for which in (2, 3):
                        for tb in range(4):
                            ts_ = slice(tb * 512, (tb + 1) * 512)
                            p, tp = nf()
                            mm8(p[:, :], lambda kc: wml[wi][:, kc, which * 128:(which + 1) * 128], lambda kc: hT[:, kc, ts_], 8,
                                [t_wml[wi]] + t_hT[4 * tb:4 * tb + 4], [tp])
                            if which == 2:
                                evac(vT[:, ts_], p[:, :], [tp], [t_vT])
                            else:
                                ACT("activation", ogT[:, ts_], p[:, :], AF.Sigmoid, r=[tp], w=[t_ogT])
                    for tt in range(NT):
                        tsl = slice(tt * 128, (tt + 1) * 128)
                        pb_, tpb = nb()
                        PE("transpose", pb_[:, 0:128], km[:, tsl], idb[:], r=[t_km, t_c], w=[tpb])
                        evac(ktok[:, tt, :], pb_[:, 0:128], [tpb], [t_kt])
                        pb_, tpb = nb()
                        PE("transpose", pb_[:, 0:128], vT[:, tsl], idb[:], r=[t_vT, t_c], w=[tpb])
                        evac(va[:, tt, 0:128], pb_[:, 0:128], [tpb], [t_va])
                    stop(3.4)
                    DVE("memset", Ct[:], 0.0, w=[t_Ct])
                    DVE("memset", Ctb[:], 0.0, w=[t_Ctb])
                    def ml_pre(tt):
                        b = tt % 2
                        tsl = slice(tt * 128, (tt + 1) * 128)
                        DVE("tensor_copy", lfb[b][:, 0, :], lf_hi[:, tt, hd:hd + 1].to_broadcast([128, 128]), r=[t_lf], w=[t_lfb[b]])
                        DVE("tensor_copy", lfb[b][:, 1, :], lf_lo[:, tt, hd:hd + 1].to_broadcast([128, 128]), r=[t_lf], w=[t_lfb[b]])
                        pbr, tbr = nf()
                        PE.group([("matmul", (pbr[:, 0:128], lfb[b][:, 0, :], tri_b[:]), dict(start=True, stop=False)),
                                  ("matmul", (pbr[:, 0:128], lfb[b][:, 1, :], tri_b[:]), dict(start=False, stop=True))],
                                 r=[t_lfb[b], t_c], w=[tbr])
                        DVE("tensor_tensor", arg[b][:], pbr[:, 0:128], negm[:], ALU.add, r=[tbr, t_c], w=[t_DT[b]])
                        ACT("activation", DT[b][:], arg[b][:], AF.Exp, bias=ib[:, tt, hd:hd + 1], r=[t_DT[b], t_ib], w=[t_DT[b]])
                        ACT("activation", eb[b][:], pbr[:, 0:128], AF.Exp, r=[tbr], w=[t_eb[b]])
                        DVE("tensor_copy", sm[b][:, 0:1], pbr[:, 127:128], r=[tbr], w=[t_sm[b]])
                        ACT("activation", sm[b][:, 1:2], ib[:, tt, hd:hd + 1], AF.Exp, bias=sm[b][:, 0:1], r=[t_sm[b], t_ib], w=[t_sm[b]])
                        ACT("activation", sm[b][:, 2:3], sm[b][:, 0:1], AF.Exp, r=[t_sm[b]], w=[t_sm[b]])
                        pS, tS = nf()
                        PE("matmul", pS[:, 0:128], km[:, tsl], qm[:, tsl], start=True, stop=True, r=[t_km, t_qm], w=[tS])
                        DVE("scalar_tensor_tensor", sqk[b][:], pS[:, 0:128], 128.0 ** -0.5, DT[b][:], ALU.mult, ALU.mult,
                            r=[tS, t_DT[b]], w=[t_sqk[b]])
                        DVE("tensor_tensor", qsc[b][:], qm[:, tsl], eb[b][:], ALU.mult, r=[t_qm, t_eb[b]], w=[t_qsc[b]])
                        DVE("tensor_scalar", vwt[b][:], va[:, tt, :], sm[b][:, 1:2], 128.0 ** -0.5, ALU.mult, ALU.mult,
                            r=[t_va, t_sm[b]], w=[t_vwt[b]])

                    def ml_seq(tt):
                        b = tt % 2
                        tsl = slice(tt * 128, (tt + 1) * 128)
                        pN, tN = nf()
                        PE.group([("matmul", (pN[:, 0:129], sqk[b][:], va[:, tt, :]), dict(start=True, stop=False)),
                                  ("matmul", (pN[:, 0:129], qsc[b][:], Ctb[:]), dict(start=False, stop=True))],
                                 r=[t_sqk[b], t_va, t_qsc[b], t_Ctb], w=[tN])
                        pU, tU = nf()
                        PE("matmul", pU[:, 0:129], ktok[:, tt, :], vwt[b][:], start=True, stop=True, r=[t_kt, t_vwt[b]], w=[tU])
                        DVE("scalar_tensor_tensor", Ct[:], Ct[:], sm[b][:, 2:3], pU[:, 0:129], ALU.mult, ALU.add,
                            r=[tU, t_sm[b]], w=[t_Ct])
                        DVE("tensor_copy", Ctb[:], Ct[:], r=[t_Ct], w=[t_Ctb])
                        DVE("tensor_scalar", hst[:, 0:1], pN[:, 128:129], -1.0, None, ALU.mult, r=[tN], w=[t_h])
                        DVE("tensor_tensor", hst[:, 0:1], hst[:, 0:1], pN[:, 128:129], ALU.max, r=[tN], w=[t_h])
                        DVE("tensor_scalar_max", hst[:, 0:1], hst[:, 0:1], 1.0, w=[t_h])
                        DVE("reciprocal", hst[:, 0:1], hst[:, 0:1], w=[t_h])
                        DVE("tensor_scalar", hnm[:], pN[:, 0:128], hst[:, 0:1], None, ALU.mult, r=[tN], w=[t_h])
                        DVE("tensor_tensor", hsq[:], hnm[:], hnm[:], ALU.mult, w=[t_h])
                        DVE("tensor_reduce", hst[:, 1:2], hsq[:], AX.X, ALU.add, w=[t_h])
                        DVE("tensor_scalar", hst[:, 1:2], hst[:, 1:2], 1.0 / 128, EPS, ALU.mult, ALU.add, w=[t_h])
                        ACT("activation", hst[:, 2:3], hst[:, 1:2], AF.Ln, r=[t_h], w=[t_h])
                        ACT("activation", hst[:, 3:4], hst[:, 2:3], AF.Exp, scale=-0.5, r=[t_h], w=[t_h])
                        DVE("scalar_tensor_tensor", hsq[:], hnm[:], hst[:, 3:4], hgb[:, hd * 128:(hd + 1) * 128],
                            ALU.mult, ALU.mult, r=[t_h, t_mc], w=[t_h])
                        DVE("tensor_copy", yml[:], hsq[:], r=[t_h], w=[t_ymlb])
                        pb_, tpb = nb()
                        PE("transpose", pb_[:, 0:128], yml[:], idb[:], r=[t_ymlb, t_c], w=[tpb])
                        DVE("tensor_tensor", y_mlT[:, hd, tsl], pb_[:, 0:128], ogT[:, tsl], ALU.mult, r=[tpb, t_ogT], w=[t_yml])

                    ml_pre(0)
                    for tt in range(NT):
                        if tt + 1 < NT:
                            ml_pre(tt + 1)
                        ml_seq(tt)
                        if tt == 1:
                            stop(3.5)
                K.barrier()
            if debug:
                POOL.dma(dbg["d_ymlT"], y_mlT[:].rearrange("p a b -> p (a b)"), trk("dbg2"), r=[t_yml])
            stop(3)

            y_memT = sb(sB, "y_memT", [128, 4, S], BF16)
            t_ymem = trk("ymemT")
            with ExitStack() as ph:
                gb = sb(ph, "gb4", [128, D], F32)
                t_gb = trk("gb4")
                SP.dma(gb[:], g_mem.partition_broadcast(128), t_gb, w=[t_gb])
                wmkv = sb(ph, "wmkv", [128, 8, 1024], BF16)
                t_wm = trk("wmkv")
                wmr = w_mem_kv.rearrange("(kc p) n -> p kc n", p=128)
                for c in range(2):
                    POOL.dma(wmkv[:, :, c * 512:(c + 1) * 512], wmr[:, :, c * 512:(c + 1) * 512], t_wm, w=[t_wm])
                wqm = sb(ph, "wqm", [128, 8, 512], BF16)
                POOL.dma(wqm[:], w_in_r[:, :, OFF_QM:OFF_QM + 512], t_wm, w=[t_wm])
                mt_ = [sb(ph, f"mt{i}", [128, D], F32) for i in range(2)]
                sq = sb(ph, "sq4", [128, D], F32)
                mn = [sb(ph, f"mn{i}", [128, D], BF16) for i in range(2)]
                st = [sb(ph, f"st4{i}", [128, 2], F32) for i in range(2)]
                t_m = [trk(f"mem{i}") for i in range(2)]
                memT = sb(ph, "memT", [128, 8, 256], BF16)
                t_memT = trk("memT")
                for i in range(2):
                    SP.dma(mt_[i][:], mem[i * 128:(i + 1) * 128, :], t_m[i], w=[t_m[i]])
                    ACT("activation", sq[:], mt_[i][:], AF.Square, accum_out=st[i][:, 0:1], r=[t_m[i]], w=[t_m[i]])
                    rms_rstd(st[i], t_m[i], D)
                    DVE("scalar_tensor_tensor", mn[i][:], mt_[i][:], st[i][:, 1:2], gb[:], ALU.mult, ALU.mult,
                        r=[t_m[i], t_gb], w=[t_m[i]])
                    to_T(mn[i], t_m[i], lambda i=i: memT[:, :, i * 128:(i + 1) * 128], t_memT, 8)
                KmT = sb(ph, "KmT", [128, 4, 256], BF16)
                Vma = sb(ph, "Vma", [128, 2, 4, 129], BF16)
                t_kv = trk("memkv")
                DVE("memset", Vma[:, :, :, 128:129], 1.0, w=[t_kv])
                for h in range(4):
                    p, tp = nf()
                    mm8(p[:, 0:256], lambda kc: wmkv[:, kc, h * 128:(h + 1) * 128], lambda kc: memT[:, kc, :], 8,
                        [t_wm, t_memT], [tp])
                    evac(KmT[:, h, :], p[:, 0:256], [tp], [t_kv])
                for m2 in range(2):
                    p, tp = nf()
                    mm8(p[:, :], lambda kc: memT[:, kc, m2 * 128:(m2 + 1) * 128], lambda kc: wmkv[:, kc, 512:1024], 8,
                        [t_wm, t_memT], [tp])
                    evac(Vma[:, m2, :, 0:128], p[:].rearrange("p (a b) -> p a b", a=4), [tp], [t_kv])
                qmT = sb(ph, "qmT", [128, S], BF16)
                t_qmT = trk("qmT")
                Pm = [sb(ph, f"Pm{i}", [128, 512], BF16) for i in range(4)]
                t_Pm = [trk(f"Pm{i}") for i in range(4)]
                rcm_ = [sb(ph, f"rcm{i}", [128, 1], F32) for i in range(2)]
                ymb_ = [sb(ph, f"ymb{i}", [128, 128], BF16) for i in range(2)]
                t_ymb_ = [trk(f"ymb{i}") for i in range(2)]
                for h in range(4):
                    for tb in range(4):
                        ts_ = slice(tb * 512, (tb + 1) * 512)
                        p, tp = nf()
                        mm8(p[:, :], lambda kc: wqm[:, kc, h * 128:(h + 1) * 128], lambda kc: hT[:, kc, ts_], 8,
                            [t_wm] + t_hT[4 * tb:4 * tb + 4], [tp])
                        evac(qmT[:, ts_], p[:, :], [tp], [t_qmT], scale=128.0 ** -0.5)
                    for tb in range(4):
                        ts_ = slice(tb * 512, (tb + 1) * 512)
                        pb2 = (tb % 2) * 2
                        for m2 in range(2):
                            pS, tS = nf()
                            PE("matmul", pS[:, :], KmT[:, h, m2 * 128:(m2 + 1) * 128], qmT[:, ts_], start=True, stop=True,
                               r=[t_kv, t_qmT], w=[tS])
                            ACT("activation", Pm[pb2 + m2][:], pS[:, :], AF.Exp, r=[tS], w=[t_Pm[pb2 + m2]])
                        for q in range(4):
                            tt = tb * 4 + q
                            yb = q % 2
                            rcm, ymb, t_ymb = rcm_[yb], ymb_[yb], t_ymb_[yb]
                            pO, tO = nf()
                            PE.group([("matmul", (pO[:, 0:129], Pm[pb2 + m2][:, q * 128:(q + 1) * 128], Vma[:, m2, h, :]),
                                       dict(start=(m2 == 0), stop=(m2 == 1))) for m2 in range(2)],
                                     r=[t_Pm[pb2], t_Pm[pb2 + 1], t_kv], w=[tO])
                            DVE("reciprocal", rcm[:], pO[:, 128:129], r=[tO], w=[t_ymb])
                            DVE("tensor_scalar", ymb[:], pO[:, 0:128], rcm[:, 0:1], None, ALU.mult, r=[tO], w=[t_ymb])
                            to_T(ymb, t_ymb, lambda h=h, tt=tt: y_memT[:, h:h + 1, tt * 128:(tt + 1) * 128], t_ymem, 1)
                K.barrier()
            if debug:
                POOL.dma(dbg["d_ymemT"], y_memT[:].rearrange("p a b -> p (a b)"), trk("dbg3"), r=[t_ymem])
            stop(4)

            with ExitStack() as ph:
                wg5 = [sb(ph, f"wg5{i}", [128, 8, 3, 128], BF16) for i in range(2)]
                wp5 = [sb(ph, f"wp5{i}", [128, 16, 128], BF16) for i in range(2)]
                t_w5 = [trk(f"w5{i}") for i in range(2)]
                sgm = [sb(ph, f"sgm{i}", [128, 512], F32) for i in range(3)]
                t_sgm = [trk(f"sgm{i}") for i in range(3)]
                acc = sb(ph, "acc5", [128, 512], F32)
                tmp = sb(ph, "tmp5", [128, 512], F32)
                t_acc = trk("acc5")
                wpn_r = w_proj_nsa.rearrange("(kc p) n -> p kc n", p=128)
                wpl_r = w_proj_ml.rearrange("(kc p) n -> p kc n", p=128)
                wpm_r = w_proj_mem.rearrange("(kc p) n -> p kc n", p=128)
                for c in range(8):
                    i = c % 2
                    cs = slice(c * 128, (c + 1) * 128)
                    for b in range(3):
                        POOL.dma(wg5[i][:, :, b, :], w_in_r[:, :, OFF_GM + b * 1024 + c * 128:OFF_GM + b * 1024 + (c + 1) * 128],
                                 t_w5[i], w=[t_w5[i]])
                    POOL.dma(wp5[i][:, 0:8, :], wpn_r[:, :, cs], t_w5[i], w=[t_w5[i]])
                    POOL.dma(wp5[i][:, 8:12, :], wpl_r[:, :, cs], t_w5[i], w=[t_w5[i]])
                    POOL.dma(wp5[i][:, 12:16, :], wpm_r[:, :, cs], t_w5[i], w=[t_w5[i]])
                    for tb in range(4):
                        ts_ = slice(tb * 512, (tb + 1) * 512)
                        rh = [t_w5[i]] + t_hT[4 * tb:4 * tb + 4]
                        for b in range(3):
                            p, tp = nf()
                            mm8(p[:, :], lambda kc: wg5[i][:, kc, b, :], lambda kc: hT[:, kc, ts_], 8, rh, [tp])
                            ACT("activation", sgm[b][:], p[:, :], AF.Sigmoid, r=[tp], w=[t_sgm[b]])
                        p0, tp0 = nf()
                        mm8(p0[:, :], lambda kc: wp5[i][:, kc, :], lambda kc: y_nsaT[:, kc, ts_], 8, [t_w5[i], t_ynsa], [tp0])
                        p1, tp1 = nf()
                        mm8(p1[:, :], lambda kc: wp5[i][:, 8 + kc, :], lambda kc: y_mlT[:, kc, ts_], 4, [t_w5[i], t_yml], [tp1])
                        p2, tp2 = nf()
                        mm8(p2[:, :], lambda kc: wp5[i][:, 12 + kc, :], lambda kc: y_memT[:, kc, ts_], 4, [t_w5[i], t_ymem], [tp2])
                        DVE("tensor_tensor", acc[:], p0[:, :], sgm[0][:], ALU.mult, r=[tp0, t_sgm[0]], w=[t_acc])
                        DVE("tensor_tensor", tmp[:], p1[:, :], sgm[1][:], ALU.mult, r=[tp1, t_sgm[1]], w=[t_acc])
                        DVE("tensor_tensor", acc[:], acc[:], tmp[:], ALU.add, w=[t_acc])
                        DVE("tensor_tensor", tmp[:], p2[:, :], sgm[2][:], ALU.mult, r=[tp2, t_sgm[2]], w=[t_acc])
                        DVE("tensor_tensor", yT[:, c, ts_], acc[:], tmp[:], ALU.add, r=[t_acc], w=[t_yT])
                K.barrier()
        if debug:
            POOL.dma(dbg["d_yT"], yT[:].rearrange("p a b -> p (a b)"), trk("dbg4"), r=[t_yT])
            K.barrier()
        stop(5)

        h2T = sb(es, "h2T", [128, 8, S], BF16)
        t_h2T = [trk(f"h2T{i}") for i in range(NT)]
        t_out = [trk(f"out{i}") for i in range(NT)]
        with ExitStack() as ph:
            wo = sb(ph, "wo", [128, 8, 1024], BF16)
            t_wo = trk("wo")
            wor = w_out.rearrange("(kc p) n -> p kc n", p=128)
            for c in range(2):
                POOL.dma(wo[:, :, c * 512:(c + 1) * 512], wor[:, :, c * 512:(c + 1) * 512], t_wo, w=[t_wo])
            gb = sb(ph, "gb6", [128, D], F32)
            gb2 = sb(ph, "gb6b", [128, D], F32)
            t_gb = trk("gb6")
            SP.dma(gb[:], g_post_mix.partition_broadcast(128), t_gb, w=[t_gb])
            SP.dma(gb2[:], g_pre_ffn.partition_broadcast(128), t_gb, w=[t_gb])
            xt = [sb(ph, f"xt6{i}", [128, D], F32) for i in range(2)]
            t_xt = [trk(f"xt6{i}") for i in range(2)]
            x1 = [sb(ph, f"x16{i}", [128, D], F32) for i in range(2)]
            t_x1 = [trk(f"x16{i}") for i in range(2)]
            sq = sb(ph, "sq6", [128, D], F32)
            t_sq = trk("sq6")
            tm6 = sb(ph, "tm6", [128, D], F32)
            t_tm = trk("tm6")
            hn = [sb(ph, f"hn6{i}", [128, D], BF16) for i in range(2)]
            t_hn = [trk(f"hn6{i}") for i in range(2)]
            st = [sb(ph, f"st6{i}", [128, 4], F32) for i in range(2)]
            t_st = [trk(f"st6{i}") for i in range(2)]
            SP.dma(xt[0][:], x[0:128, :], t_xt[0], w=[t_xt[0]])
            for tt in range(NT):
                i = tt % 2
                tsl = slice(tt * 128, (tt + 1) * 128)
                if tt + 1 < NT:
                    SP.dma(xt[1 - i][:], x[(tt + 1) * 128:(tt + 2) * 128, :], t_xt[1 - i], w=[t_xt[1 - i]])
                pu = []
                for hf in range(2):
                    p, tp = nf()
                    mm8(p[:, :], lambda kc: yT[:, kc, tsl], lambda kc: wo[:, kc, hf * 512:(hf + 1) * 512], 8, [t_yT, t_wo], [tp])
                    ACT("activation", sq[:, hf * 512:(hf + 1) * 512], p[:, :], AF.Square, accum_out=st[i][:, 2 + hf:3 + hf],
                        r=[tp], w=[t_sq, t_st[i]])
                    pu.append((p, tp))
                DVE("tensor_tensor", st[i][:, 0:1], st[i][:, 2:3], st[i][:, 3:4], ALU.add, r=[t_st[i]], w=[t_st[i]])
                rms_rstd(st[i], t_st[i], D)
                for hf in range(2):
                    p, tp = pu[hf]
                    hs = slice(hf * 512, (hf + 1) * 512)
                    DVE("scalar_tensor_tensor", tm6[:, hs], p[:, :], st[i][:, 1:2], gb[:, hs], ALU.mult, ALU.mult,
                        r=[tp, t_st[i], t_gb], w=[t_tm])
                DVE("tensor_tensor", x1[i][:], tm6[:], xt[i][:], ALU.add, r=[t_tm, t_xt[i]], w=[t_x1[i]])
                SP.dma(out[tsl, :], x1[i][:], t_x1[i], r=[t_x1[i]], w=[t_out[tt]])
                if debug:
                    SP.dma(dbg["d_x1"][tsl, :], x1[i][:], t_x1[i], r=[t_x1[i]])
                ACT("activation", sq[:], x1[i][:], AF.Square, accum_out=st[i][:, 0:1], r=[t_x1[i]], w=[t_sq, t_st[i]])
                rms_rstd(st[i], t_st[i], D)
                DVE("scalar_tensor_tensor", hn[i][:], x1[i][:], st[i][:, 1:2], gb2[:], ALU.mult, ALU.mult,
                    r=[t_x1[i], t_st[i], t_gb], w=[t_hn[i]])
                to_T(hn[i], t_hn[i], lambda tsl=tsl: h2T[:, :, tsl], t_h2T[tt], 8)
            K.barrier()

        with ExitStack() as ph:
            wd = sb(ph, "wd", [128, NF, 1024], BF16)
            t_wd = trk("wd")
            wdr = w_ffn_down.rearrange("(f p) n -> p f n", p=128)
            wfr = w_ffn_in.rearrange("(kc p) n -> p kc n", p=128)
            wf = [sb(ph, f"wf{i}", [128, 8, 2, 256], BF16) for i in range(2)]
            t_wf = [trk(f"wf{i}") for i in range(2)]
            aT = sb(ph, "aT", [128, NF, 1024], BF16)
            t_aT = trk("aT")
            sl = [sb(ph, f"sl{i}", [128, 512], F32) for i in range(2)]
            t_sl = [trk(f"sl{i}") for i in range(2)]
            gb = sb(ph, "gb7", [128, D], F32)
            t_gb = trk("gb7")
            SP.dma(gb[:], g_post_ffn.partition_broadcast(128), t_gb, w=[t_gb])
            xt = [sb(ph, f"xt7{i}", [128, D], F32) for i in range(2)]
            t_xt = [trk(f"xt7{i}") for i in range(2)]
            sq = sb(ph, "sq7", [128, D], F32)
            t_sq = trk("sq7")
            tm7 = sb(ph, "tm7", [128, D], F32)
            t_tm = trk("tm7")
            fo = [sb(ph, f"fo{i}", [128, D], F32) for i in range(2)]
            t_fo = [trk(f"fo{i}") for i in range(2)]
            st = [sb(ph, f"st7{i}", [128, 4], F32) for i in range(2)]
            t_st = [trk(f"st7{i}") for i in range(2)]
            t_fin = trk("final")
            nw = 0
            nsl = 0
            for hf in range(2):
                for f in range(NF):
                    f2 = f % 2
                    if f2 == 0:
                        i = nw % 2
                        nw += 1
                        POOL.dma(wf[i][:, :, 0, :], wfr[:, :, f * 128:(f + 2) * 128], t_wf[i], w=[t_wf[i]])
                        POOL.dma(wf[i][:, :, 1, :], wfr[:, :, DFF + f * 128:DFF + (f + 2) * 128], t_wf[i], w=[t_wf[i]])
                    if hf == 0:
                        POOL.dma(wd[:, f, :], wdr[:, f, :], t_wd, w=[t_wd])
                    for tbl in range(2):
                        tb = hf * 2 + tbl
                        ts_ = slice(tb * 512, (tb + 1) * 512)
                        rh = [t_wf[i]] + t_h2T[4 * tb:4 * tb + 4]
                        pg, tpg = nf()
                        mm8(pg[:, :], lambda kc: wf[i][:, kc, 0, f2 * 128:(f2 + 1) * 128], lambda kc: h2T[:, kc, ts_], 8, rh, [tpg])
                        pu_, tpu = nf()
                        mm8(pu_[:, :], lambda kc: wf[i][:, kc, 1, f2 * 128:(f2 + 1) * 128], lambda kc: h2T[:, kc, ts_], 8, rh, [tpu])
                        j = nsl % 2
                        nsl += 1
                        ACT("activation", sl[j][:], pg[:, :], AF.Silu, r=[tpg], w=[t_sl[j]])
                        DVE("tensor_tensor", aT[:, f, tbl * 512:(tbl + 1) * 512], sl[j][:], pu_[:, :], ALU.mult,
                            r=[t_sl[j], tpu], w=[t_aT])
                t0_ = hf * 8
                SP.dma(xt[t0_ % 2][:], out[t0_ * 128:(t0_ + 1) * 128, :], t_xt[t0_ % 2], r=[t_out[t0_]], w=[t_xt[t0_ % 2]])
                for ttl in range(8):
                    tt = hf * 8 + ttl
                    i = tt % 2
                    tsl = slice(tt * 128, (tt + 1) * 128)
                    if ttl + 1 < 8:
                        SP.dma(xt[1 - i][:], out[(tt + 1) * 128:(tt + 2) * 128, :], t_xt[1 - i], r=[t_out[tt + 1]], w=[t_xt[1 - i]])
                    pu = []
                    for h2 in range(2):
                        p, tp = nf()
                        mm8(p[:, :], lambda f: aT[:, f, ttl * 128:(ttl + 1) * 128], lambda f: wd[:, f, h2 * 512:(h2 + 1) * 512],
                            NF, [t_aT, t_wd], [tp])
                        ACT("activation", sq[:, h2 * 512:(h2 + 1) * 512], p[:, :], AF.Square, accum_out=st[i][:, 2 + h2:3 + h2],
                            r=[tp], w=[t_sq, t_st[i]])
                        pu.append((p, tp))
                    DVE("tensor_tensor", st[i][:, 0:1], st[i][:, 2:3], st[i][:, 3:4], ALU.add, r=[t_st[i]], w=[t_st[i]])
                    rms_rstd(st[i], t_st[i], D)
                    for h2 in range(2):
                        p, tp = pu[h2]
                        hs = slice(h2 * 512, (h2 + 1) * 512)
                        DVE("scalar_tensor_tensor", tm7[:, hs], p[:, :], st[i][:, 1:2], gb[:, hs], ALU.mult, ALU.mult,
                            r=[tp, t_st[i], t_gb], w=[t_tm])
                    DVE("tensor_tensor", fo[i][:], tm7[:], xt[i][:], ALU.add, r=[t_tm, t_xt[i]], w=[t_fo[i]])
                    SP.dma(out[tsl, :], fo[i][:], t_fo[i], r=[t_fo[i]], w=[t_out[tt], t_fin])
            K.barrier()
    except _Stop:
        pass
    return nc


def _consts():
    c = {}
    c["c_ident"] = np.eye(128, dtype=np.float32)
    s = np.arange(128)
    tri = (s[:, None] <= s[None, :]).astype(np.float32)
    c["c_tri"] = tri
    c["c_trin"] = (1.0 - tri).astype(np.float32)
    c["c_negm"] = ((1.0 - tri) * -1e4).astype(np.float32)
    n = np.arange(128)
    t = np.arange(S)
    cm = ((n[:, None] * 16 + 31) <= t[None, :]).astype(np.float32)
    cm[127, :] = 0
    c["c_cmpmask"] = cm
    c0 = np.arange(128) * 16
    s0 = np.arange(32) * 64
    ov = np.minimum(c0[:, None] + 32, s0[None, :] + 64) - np.maximum(c0[:, None], s0[None, :])
    cmap = np.clip(ov, 0, None).astype(np.float32) / 32
    cmap[127, :] = 0
    c["c_cmap"] = cmap
    selm = np.zeros((128, 3, 16, 32), np.float32)
    for qt in range(16):
        for q in range(128):
            qb = (qt * 128 + q) // 64
            j = np.arange(32)
            rel = qb - j
            causal = rel >= 0
            forced = causal & ((j == 0) | (rel < 2))
            selm[q, 0, qt] = (causal & ~forced)
            selm[q, 1, qt] = np.where(forced, BIG, np.where(causal, 0.0, -BIG))
            selm[q, 2, qt] = causal
    c["c_selm"] = selm.reshape(128, -1)
    E = np.zeros((32, 16, 128), np.float32)
    for kt in range(16):
        for k in range(128):
            E[2 * kt + k // 64, kt, k] = 1.0
    c["c_E"] = E.reshape(32, -1)
    return c


_NC = {}


def _get_nc(debug=False):
    return build(debug)


def make_in_maps(inputs, cores):
    cst = _consts()
    maps = []
    for b in cores:
        m = dict(cst)
        for k, v in inputs.items():
            v = np.asarray(v, dtype=np.float32)
            if k in ("x", "mem"):
                m[k] = np.ascontiguousarray(v[b])
            else:
                a = v[0]
                if a.ndim == 1:
                    a = a[None, :]
                m[k] = np.ascontiguousarray(a)
        maps.append(m)
    return maps


def kernel(**inputs):
    nc = _get_nc(False)
    maps = make_in_maps(inputs, list(range(8)))
    res = run_bass_kernel_spmd(nc, maps, core_ids=list(range(8)))
    return np.stack([np.asarray(r["out"], dtype=np.float32) for r in res.results], axis=0)
```

```python
import numpy as np
from contextlib import ExitStack
import concourse.bass as bass
import concourse.mybir as mybir
from concourse.bass_utils import run_bass_kernel_spmd

F32 = mybir.dt.float32
BF16 = mybir.dt.bfloat16
AF = mybir.ActivationFunctionType
ALU = mybir.AluOpType
AX = mybir.AxisListType

S, D, NT = 2048, 1024, 16
IN_W = 8248
OFF_KV, OFF_GN, OFF_ML, OFF_IF, OFF_O, OFF_QM, OFF_GM = 1024, 2560, 2608, 4144, 4152, 4664, 5176
DFF = 2816
NF = 22
EPS = 1e-6
BIG = 1e30


class Trk:
    __slots__ = ("name", "w", "r", "dsem", "dcnt")

    def __init__(self, name):
        self.name = name
        self.w = {}
        self.r = {}
        self.dsem = None
        self.dcnt = 0


def _upd(d, sem, cnt):
    k = id(sem)
    if k not in d or d[k][1] < cnt:
        d[k] = (sem, cnt)


class Eng:
    def __init__(self, K, e, name):
        self.K = K
        self.e = e
        self.name = name
        self.sem = K.es.enter_context(K.nc.semaphore("s_" + name))
        self.cnt = 0
        self.seen = {}

    def _sync(self, r, w):
        need = {}
        for t in r:
            for s, (h, v) in t.w.items():
                if need.get(s, (None, -1))[1] < v:
                    need[s] = (h, v)
        for t in w:
            for d in (t.w, t.r):
                for s, (h, v) in d.items():
                    if need.get(s, (None, -1))[1] < v:
                        need[s] = (h, v)
        for s, (h, v) in need.items():
            if s == id(self.sem) and self.name in ("pe", "pool", "sp"):
                continue
            if self.seen.get(s, -1) >= v:
                continue
            self.e.wait_ge(h, v)
            self.seen[s] = v

    def _post(self, inst, r, w):
        self.cnt += 1
        inst.then_inc(self.sem, 1)
        for t in r:
            _upd(t.r, self.sem, self.cnt)
        for t in w:
            _upd(t.w, self.sem, self.cnt)

    def __call__(self, fname, *args, r=(), w=(), **kw):
        self._sync(r, w)
        inst = getattr(self.e, fname)(*args, **kw)
        self._post(inst, r, w)
        return inst

    def group(self, calls, r=(), w=()):
        self._sync(r, w)
        inst = None
        for fname, args, kw in calls:
            inst = getattr(self.e, fname)(*args, **kw)
        self._post(inst, r, w)

    def dma(self, out, in_, st, r=(), w=(), **kw):
        self._sync(r, w)
        K = self.K
        if st.dsem is None:
            st.dsem = K.es.enter_context(K.nc.semaphore("d_" + st.name))
        inst = self.e.dma_start(out=out, in_=in_, **kw)
        st.dcnt += 16
        inst.then_inc(st.dsem, 16)
        for t in r:
            _upd(t.r, st.dsem, st.dcnt)
        for t in w:
            _upd(t.w, st.dsem, st.dcnt)


class Kern:
    def __init__(self, nc, es):
        self.nc = nc
        self.es = es
        self.PE = Eng(self, nc.tensor, "pe")
        self.ACT = Eng(self, nc.scalar, "act")
        self.DVE = Eng(self, nc.vector, "dve")
        self.POOL = Eng(self, nc.gpsimd, "pool")
        self.SP = Eng(self, nc.sync, "sp")
        self.n = 0
        self.alltrk = []

    def trk(self, name=None):
        self.n += 1
        t = Trk(name or f"t{self.n}")
        self.alltrk.append(t)
        return t

    def barrier(self):
        engs = [self.PE, self.ACT, self.DVE, self.POOL, self.SP]
        for e in engs:
            e._sync(self.alltrk, self.alltrk)


def bc(ap, axis, shape):
    return ap.unsqueeze(axis).to_broadcast(list(shape))


import os
DBG_BR = int(os.environ['DBG_BR']) if 'DBG_BR' in os.environ else None


PAD_DEFAULT = 0


class _Stop(Exception):
    pass


def build(debug=False, upto=7):
    nc = bass.Bass("TRN2", target_bir_lowering=False)

    def din(name, shape):
        return nc.dram_tensor(name, list(shape), F32, kind="ExternalInput").ap()

    x = din("x", [S, D])
    mem = din("mem", [256, D])
    g_pre_mix = din("g_pre_mix", [1, D])
    w_in = din("w_in", [D, IN_W])
    cmp_pe_k = din("cmp_pe_k", [32, 64])
    cmp_w1_k = din("cmp_w1_k", [32, 64, 128])
    cmp_w2_k = din("cmp_w2_k", [128, 64])
    cmp_pe_v = din("cmp_pe_v", [32, 64])
    cmp_w1_v = din("cmp_w1_v", [32, 64, 128])
    cmp_w2_v = din("cmp_w2_v", [128, 64])
    ml_conv_w = din("ml_conv_w", [4, 1024])
    ml_conv_b = din("ml_conv_b", [1, 1024])
    ml_gate_b = din("ml_gate_b", [1, 8])
    ml_head_g = din("ml_head_g", [1, 512])
    g_mem = din("g_mem", [1, D])
    w_mem_kv = din("w_mem_kv", [D, 1024])
    w_proj_nsa = din("w_proj_nsa", [1024, D])
    w_proj_ml = din("w_proj_ml", [512, D])
    w_proj_mem = din("w_proj_mem", [512, D])
    w_out = din("w_out", [D, D])
    g_post_mix = din("g_post_mix", [1, D])
    g_pre_ffn = din("g_pre_ffn", [1, D])
    w_ffn_in = din("w_ffn_in", [D, 2 * DFF])
    w_ffn_down = din("w_ffn_down", [DFF, D])
    g_post_ffn = din("g_post_ffn", [1, D])
    c_ident = din("c_ident", [128, 128])
    c_tri = din("c_tri", [128, 128])
    c_trin = din("c_trin", [128, 128])
    c_negm = din("c_negm", [128, 128])
    c_cmpmask = din("c_cmpmask", [128, S])
    c_cmap = din("c_cmap", [128, 32])
    c_selm = din("c_selm", [128, 3 * 16 * 32])
    c_E = din("c_E", [32, 16 * 128])
    out = nc.dram_tensor("out", [S, D], F32, kind="ExternalOutput").ap()
    dbg = {}
    if debug:
        for nm, shp in (("d_ynsaT", [128, 8 * S]), ("d_ymlT", [128, 4 * S]), ("d_ymemT", [128, 4 * S]),
                        ("d_yT", [128, 8 * S]), ("d_hT", [128, 8 * S]), ("d_x1", [S, D])):
            dbg[nm] = nc.dram_tensor(nm, shp, F32, kind="ExternalOutput").ap()

    w_in_r = w_in.rearrange("(kc p) n -> p kc n", p=128)

    try:
      with ExitStack() as es:
        K = Kern(nc, es)
        def ddump(name, ap, shape, r):
            if not debug:
                return
            d = nc.dram_tensor(name, list(shape), F32, kind="ExternalOutput").ap()
            POOL.dma(d, ap, trk("dd_" + name), r=r)
        def stop(k):
            if upto == k:
                if debug and 2 < k < 3:
                    POOL.dma(dbg["d_ynsaT"], y_nsaT[:].rearrange("p a b -> p (a b)"), trk("dbgs"), r=[t_ynsa])
                K.barrier()
                raise _Stop()
        PE, ACT, DVE, POOL, SP = K.PE, K.ACT, K.DVE, K.POOL, K.SP
        trk = K.trk

        def sb(scope, name, shape, dt):
            return scope.enter_context(nc.sbuf_tensor(name, list(shape), dt))

        psF = [es.enter_context(nc.psum_tensor(f"psF{i}", [128, 512], F32)) for i in range(6)]
        t_psF = [trk(f"psF{i}") for i in range(6)]
        psB = [es.enter_context(nc.psum_tensor(f"psB{i}", [128, 1024], BF16)) for i in range(2)]
        t_psB = [trk(f"psB{i}") for i in range(2)]
        rot = {"f": 0, "b": 0, "s": 0, "e": 0}

        def nf():
            i = rot["f"] % 6
            rot["f"] += 1
            return psF[i], t_psF[i]

        def ns():
            i = rot["s"] % 3
            rot["s"] += 1
            return psF[i], t_psF[i]

        def nb():
            i = rot["b"] % 2
            rot["b"] += 1
            return psB[i], t_psB[i]

        def evac(out_ap, in_ap, r, w, scale=None, eng=None):
            if eng is None:
                eng = rot["e"] % 2
                rot["e"] += 1
            if eng == 0:
                if scale is None:
                    ACT("copy", out_ap, in_ap, r=r, w=w)
                else:
                    ACT("mul", out_ap, in_ap, scale, r=r, w=w)
            else:
                if scale is None:
                    DVE("tensor_copy", out_ap, in_ap, r=r, w=w)
                else:
                    DVE("tensor_scalar", out_ap, in_ap, scale, None, ALU.mult, r=r, w=w)

        def mm8(p_ap, lhs_fn, rhs_fn, nk, r, w):
            calls = [("matmul", (p_ap, lhs_fn(kc), rhs_fn(kc)), dict(start=(kc == 0), stop=(kc == nk - 1)))
                     for kc in range(nk)]
            PE.group(calls, r=r, w=w)

        t_c = trk("const")
        idb = sb(es, "idb", [128, 128], BF16)
        tri_b = sb(es, "tri_b", [128, 128], BF16)
        trin_b = sb(es, "trin_b", [128, 128], BF16)
        tri_f = sb(es, "tri_f", [128, 128], F32)
        negm = sb(es, "negm", [128, 128], F32)
        POOL.dma(idb[:], c_ident, t_c, w=[t_c])
        POOL.dma(tri_b[:], c_tri, t_c, w=[t_c])
        POOL.dma(trin_b[:], c_trin, t_c, w=[t_c])
        POOL.dma(tri_f[:], c_tri, t_c, w=[t_c])
        idf = sb(es, "idf", [128, 128], F32)
        POOL.dma(idf[:], c_ident, t_c, w=[t_c])
        POOL.dma(negm[:], c_negm, t_c, w=[t_c])

        npad = int(os.environ.get("PAD", PAD_DEFAULT))
        if npad:
            padt = sb(es, "padt", [128, 128], BF16)
            t_pad = trk("pad")
            for _ in range(npad):
                DVE("memset", padt[:], 0.0, w=[t_pad])
            for _ in range(npad):
                pb_, tpb = nb()
                PE("transpose", pb_[:, 0:128], padt[:], idb[:], r=[t_pad, t_c], w=[tpb])
                ACT("copy", padt[:], pb_[:, 0:128], r=[tpb], w=[t_pad])
        yT = sb(es, "yT", [128, 8, S], BF16)
        t_yT = trk("yT")

        def rms_rstd(st, t_st, n):
            DVE("tensor_scalar", st[:, 1:2], st[:, 0:1], 1.0 / n, EPS, ALU.mult, ALU.add, r=[t_st], w=[t_st])
            ACT("activation", st[:, 1:2], st[:, 1:2], AF.Sqrt, r=[t_st], w=[t_st])
            DVE("reciprocal", st[:, 1:2], st[:, 1:2], r=[t_st], w=[t_st])

        def to_T(src_bf, t_src, dst_fn, t_dst, nchunk, eng=None):
            pb_, tpb = nb()
            calls = [("transpose", (pb_[:, kc * 128:(kc + 1) * 128], src_bf[:, kc * 128:(kc + 1) * 128], idb[:]), {})
                     for kc in range(nchunk)]
            PE.group(calls, r=[t_src, t_c], w=[tpb])
            evac(dst_fn(), pb_[:, 0:nchunk * 128].rearrange("p (a b) -> p a b", a=nchunk), r=[tpb], w=[t_dst], eng=eng)

        with ExitStack() as sB:
            hT = sb(sB, "hT", [128, 8, S], BF16)
            t_hT = [trk(f"hT{i}") for i in range(NT)]
            y_nsaT = sb(sB, "y_nsaT", [128, 8, S], BF16)
            t_ynsa = trk("ynsaT")

            with ExitStack() as ph:
                gb = sb(ph, "gb1", [128, D], F32)
                t_gb = trk("gb1")
                SP.dma(gb[:], g_pre_mix.partition_broadcast(128), t_gb, w=[t_gb])
                xt = [sb(ph, f"xt{i}", [128, D], F32) for i in range(2)]
                t_xt = [trk(f"xt{i}") for i in range(2)]
                sq = sb(ph, "sq1", [128, D], F32)
                t_sq = trk("sq1")
                hn = [sb(ph, f"hn{i}", [128, D], BF16) for i in range(2)]
                t_hn = [trk(f"hn{i}") for i in range(2)]
                st = [sb(ph, f"st{i}", [128, 2], F32) for i in range(2)]
                t_st = [trk(f"st{i}") for i in range(2)]
                for tt in range(NT):
                    i = tt % 2
                    SP.dma(xt[i][:], x[tt * 128:(tt + 1) * 128, :], t_xt[i], w=[t_xt[i]])
                    ACT("activation", sq[:], xt[i][:], AF.Square, accum_out=st[i][:, 0:1], r=[t_xt[i]], w=[t_sq, t_st[i]])
                    rms_rstd(st[i], t_st[i], D)
                    DVE("scalar_tensor_tensor", hn[i][:], xt[i][:], st[i][:, 1:2], gb[:], ALU.mult, ALU.mult,
                        r=[t_xt[i], t_st[i], t_gb], w=[t_hn[i]])
                    to_T(hn[i], t_hn[i], lambda tt=tt: hT[:, :, tt * 128:(tt + 1) * 128], t_hT[tt], 8)
                K.barrier()
            if debug:
                POOL.dma(dbg["d_hT"], hT[:].rearrange("p a b -> p (a b)"), trk("dbg0"), r=t_hT)
            stop(1)

            with ExitStack() as ph:
                cmpmask = sb(ph, "cmpmask", [128, S], BF16)
                selm = sb(ph, "selm", [128, 3, 16, 32], F32)
                Esel = sb(ph, "Esel", [32, 16, 128], BF16)
                POOL.dma(cmpmask[:], c_cmpmask, t_c, w=[t_c])
                POOL.dma(selm[:].rearrange("p a b c -> p (a b c)"), c_selm, t_c, w=[t_c])
                POOL.dma(Esel[:].rearrange("p a b -> p (a b)"), c_E, t_c, w=[t_c])
                w1 = sb(ph, "w1kv", [128, 32, 128], BF16)
                w2k = sb(ph, "w2k", [128, 64], BF16)
                w2v = sb(ph, "w2v", [128, 64], BF16)
                peT = sb(ph, "peT", [128, 32], BF16)
                t_cw = trk("cmpw")
                POOL.dma(w1[0:64], cmp_w1_k.rearrange("l d h -> d l h"), t_cw, w=[t_cw])
                POOL.dma(w1[64:128], cmp_w1_v.rearrange("l d h -> d l h"), t_cw, w=[t_cw])
                POOL.dma(w2k[:], cmp_w2_k, t_cw, w=[t_cw])
                POOL.dma(w2v[:], cmp_w2_v, t_cw, w=[t_cw])
                pe_sb = sb(ph, "pe_sb", [32, 128], F32)
                t_pe = trk("pe_sb")
                SP.dma(pe_sb[:, 0:64], cmp_pe_k, t_pe, w=[t_pe])
                SP.dma(pe_sb[:, 64:128], cmp_pe_v, t_pe, w=[t_pe])
                p, tp = nf()
                PE("transpose", p[:, 0:32], pe_sb[:], idf[0:32, 0:32], r=[t_pe, t_c], w=[tp])
                DVE("tensor_copy", peT[:], p[:, 0:32], r=[tp], w=[t_cw])
                cbias = sb(ph, "cbias", [128, 2], F32)
                t_cb = trk("cbias")
                for j in range(2):
                    p, tp = nf()
                    lo = j * 64
                    calls = [("matmul", (p[:, 0:1], w1[lo:lo + 64, l, :], peT[lo:lo + 64, l:l + 1]),
                              dict(start=(l == 0), stop=(l == 31))) for l in range(32)]
                    PE.group(calls, r=[t_cw], w=[tp])
                    DVE("tensor_copy", cbias[:, j:j + 1], p[:, 0:1], r=[tp], w=[t_cb])
                stop(2.1)

                wg = sb(ph, "wgrp", [128, 8, 652], BF16)
                t_wg = trk("wgrp")
                qTg = sb(ph, "qTg", [96, 4, S], BF16)
                t_q = trk("qTg")
                kvc = sb(ph, "kvc", [128, S], BF16)
                t_kvc = trk("kvc")
                kslcT = sb(ph, "kslcT", [96, S], BF16)
                kwinT = sb(ph, "kwinT", [64, S], BF16)
                t_ks = trk("kslcT")
                t_kw = trk("kwinT")
                vs = sb(ph, "vs", [128, NT, 65], BF16)
                vw = sb(ph, "vw", [128, NT, 65], BF16)
                t_vs = trk("vs")
                t_vw = trk("vw")
                sg = sb(ph, "sg", [128, NT, 12], F32)
                t_sg = trk("sg")
                hid = sb(ph, "hid", [128, 2, 128], BF16)
                t_hid = trk("hid")
                kcT = sb(ph, "kcT", [64, 128], BF16)
                t_kcT = trk("kcT")
                vca = sb(ph, "vca", [128, 98], BF16)
                t_vca = trk("vca")
                PcT = sb(ph, "PcT", [128, 512], BF16)
                t_PcT = trk("PcT")
                Pt = [sb(ph, f"Pt{i}", [128, 512], BF16) for i in range(3)]
                t_Pt = [trk(f"Pt{i}") for i in range(3)]
                rc = sb(ph, "rc", [128, 4], F32)
                t_rc = trk("rc")
                impt = sb(ph, "impt", [128, 4, 32], F32)
                imp = sb(ph, "imp", [128, 32], F32)
                sc = sb(ph, "sc", [128, 32], F32)
                cmpb = sb(ph, "cmpb", [128, 32, 32], BF16)
                rank = sb(ph, "rank", [128, 32], F32)
                sel = sb(ph, "sel", [128, 32], F32)
                biasq = sb(ph, "biasq", [128, 96], BF16)
                t_sel = trk("selwork")
                t_bq = trk("biasq")
                biasT = sb(ph, "biasT", [32, 4, 128], BF16)
                t_bT = trk("biasT")
                fac = sb(ph, "fac", [128, 3, 4], F32)
                t_fac = trk("fac")
                yacc = sb(ph, "yacc", [128, 4, 64], F32)
                ytmp = sb(ph, "ytmp", [128, 4, 64], F32)
                ytmp2 = sb(ph, "ytmp2", [128, 4, 64], F32)
                ybf = sb(ph, "ybf", [128, 256], BF16)
                t_ya = trk("yacc")
                t_ybf = trk("ybf")
                DVE("memset", vs[:, :, 64:65], 1.0, w=[t_vs])
                DVE("memset", vw[:, :, 64:65], 1.0, w=[t_vw])
                DVE("memset", vca[:], 0.0, w=[t_vca])
                DVE("memset", vca[:, 64:65], 1.0, w=[t_vca])
                POOL.dma(vca[:, 66:98], c_cmap, t_c, w=[t_c, t_vca])
                DVE("memset", hid[:], 0.0, w=[t_hid])
                DVE("memset", kcT[:], 0.0, w=[t_kcT])
                DVE("memset", biasq[:], 0.0, w=[t_bq])
                POOL.dma(kslcT[64:96, :], c_E, t_c, w=[t_c, t_ks])

                for g in range(4):
                    POOL.dma(wg[:, :, 0:256], w_in_r[:, :, g * 256:(g + 1) * 256], t_wg, w=[t_wg])
                    for si, slot in enumerate((0, 1, 2, 4, 3, 5)):
                        c0 = OFF_KV + slot * 256 + g * 64
                        POOL.dma(wg[:, :, 256 + si * 64:256 + (si + 1) * 64], w_in_r[:, :, c0:c0 + 64], t_wg, w=[t_wg])
                    POOL.dma(wg[:, :, 640:652], w_in_r[:, :, OFF_GN + g * 12:OFF_GN + (g + 1) * 12], t_wg, w=[t_wg])
                    for tb in range(4):
                        ts_ = slice(tb * 512, (tb + 1) * 512)
                        rr = [t_wg] + t_hT[4 * tb:4 * tb + 4]
                        for hh in range(4):
                            p, tp = nf()
                            mm8(p[0:64, :], lambda kc: wg[:, kc, hh * 64:(hh + 1) * 64], lambda kc: hT[:, kc, ts_], 8, rr, [tp])
                            evac(qTg[0:64, hh, ts_], p[0:64, :], [tp], [t_q], scale=0.125)
                        p, tp = nf()
                        mm8(p[:, :], lambda kc: wg[:, kc, 256:384], lambda kc: hT[:, kc, ts_], 8, rr, [tp])
                        evac(kvc[:, ts_], p[:, :], [tp], [t_kvc])
                        p, tp = nf()
                        mm8(p[0:64, :], lambda kc: wg[:, kc, 384:448], lambda kc: hT[:, kc, ts_], 8, rr, [tp])
                        evac(kslcT[0:64, ts_], p[0:64, :], [tp], [t_ks])
                        p, tp = nf()
                        mm8(p[0:64, :], lambda kc: wg[:, kc, 448:512], lambda kc: hT[:, kc, ts_], 8, rr, [tp])
                        evac(kwinT[:, ts_], p[0:64, :], [tp], [t_kw])
                    for tt in range(NT):
                        tsl = slice(tt * 128, (tt + 1) * 128)
                        p, tp = nf()
                        calls = []
                        for kc in range(8):
                            calls.append(("matmul", (p[:, 0:128], hT[:, kc, tsl], wg[:, kc, 512:640]),
                                          dict(start=(kc == 0), stop=(kc == 7))))
                        for kc in range(8):
                            calls.append(("matmul", (p[:, 128:140], hT[:, kc, tsl], wg[:, kc, 640:652]),
                                          dict(start=(kc == 0), stop=(kc == 7))))
                        PE.group(calls, r=[t_wg, t_hT[tt]], w=[tp])
                        DVE("tensor_copy", vs[:, tt, 0:64], p[:, 0:64], r=[tp], w=[t_vs])
                        DVE("tensor_copy", vw[:, tt, 0:64], p[:, 64:128], r=[tp], w=[t_vw])
                        ACT("activation", sg[:, tt, :], p[:, 128:140], AF.Sigmoid, r=[tp], w=[t_sg])
                    stop(2.2)
                    for j in range(2):
                        lo = j * 64
                        p, tp = nf()
                        calls = [("matmul", (p[:, 0:127], w1[lo:lo + 64, l, :], kvc[lo:lo + 64, l:l + 2017:16]),
                                  dict(start=(l == 0), stop=(l == 31))) for l in range(32)]
                        PE.group(calls, r=[t_cw, t_kvc], w=[tp])
                        ACT("activation", hid[:, j, 0:127], p[:, 0:127], AF.Silu, bias=cbias[:, j:j + 1],
                            r=[tp, t_cb], w=[t_hid])
                    p, tp = nf()
                    PE("matmul", p[0:64, 0:127], w2k[:], hid[:, 0, 0:127], start=True, stop=True, r=[t_cw, t_hid], w=[tp])
                    DVE("tensor_copy", kcT[:, 0:127], p[0:64, 0:127], r=[tp], w=[t_kcT])
                    p, tp = nf()
                    PE("matmul", p[0:127, 0:64], hid[:, 1, 0:127], w2v[:], start=True, stop=True, r=[t_cw, t_hid], w=[tp])
                    DVE("tensor_copy", vca[0:127, 0:64], p[0:127, 0:64], r=[tp], w=[t_vca])

                    if g == 0:
                        ddump("dd_hid", hid[:].rearrange("p a b -> p (a b)"), [128, 256], [t_hid])
                        ddump("dd_kcT", kcT[:], [64, 128], [t_kcT])
                        ddump("dd_vca", vca[:], [128, 98], [t_vca])
                        ddump("dd_cbias", cbias[:], [128, 2], [t_cb])
                        ddump("dd_kvc", kvc[:], [128, S], [t_kvc])
                        ddump("dd_peT", peT[:], [128, 32], [t_cw])
                    stop(2.3)
                    pOc, tOc = psF[3], t_psF[3]
                    pOs, tOs = psF[4], t_psF[4]
                    pOw, tOw = psF[5], t_psF[5]
                    pOc3 = pOc[:].rearrange("p (a b) -> p a b", a=4)
                    pOs3 = pOs[:].rearrange("p (a b) -> p a b", a=4)
                    pOw3 = pOw[:].rearrange("p (a b) -> p a b", a=4)
                    for qt in range(NT):
                        qs = slice(qt * 128, (qt + 1) * 128)
                        q4 = qTg[0:64, :, qs]
                        q4b = qTg[0:96, :, qs]
                        Pc3 = PcT[:].rearrange("p (a b) -> p a b", a=4)
                        k0 = max(0, qt - 4)
                        steps = [("c", 0)] + [("w", kt) for kt in range(k0, qt + 1)] + [("s", kt) for kt in range(qt + 1)]
                        state = {}

                        def emit_S(idx):
                            br, kt = steps[idx]
                            ks = slice(kt * 128, (kt + 1) * 128)
                            pS, tS = ns()
                            pS3 = pS[:].rearrange("p (a b) -> p a b", a=4)
                            if br == "c":
                                PE("matmul", pS3[0:127], kcT[:, 0:127], q4, start=True, stop=True, r=[t_kcT, t_q], w=[tS])
                                ACT("activation", PcT[0:127, :], pS[0:127, :], AF.Exp, r=[tS], w=[t_PcT])
                                DVE("tensor_tensor", Pc3[0:127], Pc3[0:127], bc(cmpmask[0:127, qs], 1, [127, 4, 128]), ALU.mult,
                                    r=[t_c], w=[t_PcT])
                                return
                            if br == "w":
                                PE("matmul", pS3, kwinT[:, ks], q4, start=True, stop=True, r=[t_kw, t_q], w=[tS])
                            else:
                                if qt < 8:
                                    PE("matmul", pS3, kslcT[0:64, ks], q4, start=True, stop=True, r=[t_ks, t_q], w=[tS])
                                else:
                                    PE("matmul", pS3, kslcT[0:96, ks], q4b, start=True, stop=True, r=[t_ks, t_q, t_c], w=[tS])
                            i = rot.setdefault("p", 0) % 3
                            rot["p"] += 1
                            state[idx] = i
                            ACT("activation", Pt[i][:], pS[:], AF.Exp, r=[tS], w=[t_Pt[i]])
                            P3 = Pt[i][:].rearrange("p (a b) -> p a b", a=4)
                            if kt == qt:
                                DVE("tensor_tensor", P3, P3, bc(tri_b[:], 1, [128, 4, 128]), ALU.mult, r=[t_c], w=[t_Pt[i]])
                            elif br == "w" and kt == qt - 4:
                                DVE("tensor_tensor", P3, P3, bc(trin_b[:], 1, [128, 4, 128]), ALU.mult, r=[t_c], w=[t_Pt[i]])

                        def emit_PV(idx):
                            br, kt = steps[idx]
                            if br == "c":
                                calls = [("matmul", (pOc3[:, hh, 0:98], PcT[0:127, hh * 128:(hh + 1) * 128], vca[0:127, 0:98]),
                                          dict(start=True, stop=True)) for hh in range(4)]
                                PE.group(calls, r=[t_PcT, t_vca], w=[tOc])
                                DVE("tensor_scalar_max", rc[:, 0:4], pOc3[:, :, 64], 1e-20, r=[tOc], w=[t_rc])
                                DVE("reciprocal", rc[:, 0:4], rc[:, 0:4], r=[t_rc], w=[t_rc])
                                if qt < 8:
                                    DVE("tensor_scalar", biasq[:, 64:96], selm[:, 2, qt, :], -1.0, 30000.0, ALU.add, ALU.mult,
                                        r=[t_c], w=[t_bq])
                                    return
                                DVE("tensor_tensor", impt[:], pOc3[:, :, 66:98], bc(rc[:, 0:4], 2, [128, 4, 32]), ALU.mult,
                                    r=[tOc, t_rc], w=[t_sel])
                                DVE("tensor_reduce", imp[:], impt[:].rearrange("p h j -> p j h"), AX.X, ALU.add, r=[t_sel], w=[t_sel])
                                DVE("tensor_tensor", sc[:], imp[:], selm[:, 0, qt, :], ALU.mult, r=[t_sel, t_c], w=[t_sel])
                                DVE("tensor_tensor", sc[:], sc[:], selm[:, 1, qt, :], ALU.add, r=[t_sel, t_c], w=[t_sel])
                                DVE("tensor_tensor", cmpb[:], bc(sc[:], 1, [128, 32, 32]), bc(sc[:], 2, [128, 32, 32]), ALU.is_gt,
                                    r=[t_sel], w=[t_sel])
                                DVE("tensor_reduce", rank[:], cmpb[:], AX.X, ALU.add, r=[t_sel], w=[t_sel])
                                DVE("scalar_tensor_tensor", sel[:], rank[:], 15.5, selm[:, 2, qt, :], ALU.is_lt, ALU.mult,
                                    r=[t_sel, t_c], w=[t_sel])
                                DVE("tensor_scalar", biasq[:, 64:96], sel[:], -1.0, 30000.0, ALU.add, ALU.mult, r=[t_sel], w=[t_bq])
                                return
                            i = state[idx]
                            if br == "w":
                                calls = [("matmul", (pOw3[:, hh, 0:65], Pt[i][:, hh * 128:(hh + 1) * 128], vw[:, kt, :]),
                                          dict(start=(kt == k0 and hh == 0), stop=(kt == qt))) for hh in range(4)]
                                PE.group(calls, r=[t_Pt[i], t_vw], w=[tOw])
                            else:
                                calls = [("matmul", (pOs3[:, hh, 0:65], Pt[i][:, hh * 128:(hh + 1) * 128], vs[:, kt, :]),
                                          dict(start=(kt == 0 and hh == 0), stop=(kt == qt))) for hh in range(4)]
                                PE.group(calls, r=[t_Pt[i], t_vs], w=[tOs])

                        SK = 2
                        nrem = len(steps) - 1
                        emit_S(0)
                        for j in range(nrem + SK):
                            if j < nrem:
                                if steps[1 + j] == ("s", 0):
                                    if qt >= 8:
                                        pb_, tpb = nb()
                                        PE("transpose", pb_[0:96, 0:128], biasq[:], idb[:], r=[t_bq, t_c], w=[tpb])
                                        DVE("tensor_copy", qTg[64:96, :, qs], bc(pb_[64:96, 0:128], 1, [32, 4, 128]),
                                            r=[tpb], w=[t_q])
                                emit_S(1 + j)
                            if j == 0:
                                emit_PV(0)
                            if j >= SK:
                                emit_PV(1 + j - SK)
                        DVE("tensor_copy", fac[:, 0, :], rc[:, 0:4], r=[t_rc], w=[t_fac])
                        DVE("reciprocal", fac[:, 1, :], pOs3[:, :, 64], r=[tOs], w=[t_fac])
                        DVE("reciprocal", fac[:, 2, :], pOw3[:, :, 64], r=[tOw], w=[t_fac])
                        if DBG_BR is None:
                            DVE("tensor_tensor", fac[:], fac[:], sg[:, qt, :].rearrange("p (h b) -> p b h", b=3), ALU.mult,
                                r=[t_sg], w=[t_fac])
                        else:
                            for bb in range(3):
                                if bb != DBG_BR:
                                    DVE("memset", fac[:, bb, :], 0.0, w=[t_fac])
                        DVE("tensor_tensor", yacc[:], pOc3[:, :, 0:64], bc(fac[:, 0, :], 2, [128, 4, 64]), ALU.mult,
                            r=[tOc, t_fac], w=[t_ya])
                        DVE("tensor_tensor", ytmp[:], pOs3[:, :, 0:64], bc(fac[:, 1, :], 2, [128, 4, 64]), ALU.mult,
                            r=[tOs, t_fac], w=[t_ya])
                        DVE("tensor_tensor", ytmp2[:], pOw3[:, :, 0:64], bc(fac[:, 2, :], 2, [128, 4, 64]), ALU.mult,
                            r=[tOw, t_fac], w=[t_ya])
                        DVE("tensor_tensor", yacc[:], yacc[:], ytmp[:], ALU.add, r=[t_ya], w=[t_ya])
                        DVE("tensor_tensor", ybf[:].rearrange("p (a b) -> p a b", a=4), yacc[:], ytmp2[:], ALU.add,
                            r=[t_ya], w=[t_ybf])
                        to_T(ybf, t_ybf, lambda g=g, qs=qs: y_nsaT[:, 2 * g:2 * g + 2, qs], t_ynsa, 2, eng=0)
                        if qt == 1 and g == 0 and debug:
                            dtmp = sb(ph, "dtmp", [128, 512], F32)
                            DVE("tensor_copy", dtmp[:], pOc[:], r=[tOc], w=[t_ya])
                            ddump("dd_pOc", dtmp[:], [128, 512], [t_ya])
                            ddump("dd_PcT", PcT[:], [128, 512], [t_PcT])
                            ddump("dd_rc", rc[:], [128, 4], [t_rc])
                            ddump("dd_fac", fac[:].rearrange("p a b -> p (a b)"), [128, 12], [t_fac])
                            ddump("dd_yacc", yacc[:].rearrange("p a b -> p (a b)"), [128, 256], [t_ya])
                            ddump("dd_q4", qTg[:, :, qs], [64, 4, 128], [t_q])
                            stop(2.45)
                        if qt == 0:
                            stop(2.4)
                        if qt == 5:
                            stop(2.5)
                K.barrier()
            if debug:
                POOL.dma(dbg["d_ynsaT"], y_nsaT[:].rearrange("p a b -> p (a b)"), trk("dbg1"), r=[t_ynsa])
            stop(2)

            y_mlT = sb(sB, "y_mlT", [128, 4, S], BF16)
            t_yml = trk("ymlT")
            with ExitStack() as ph:
                wif = sb(ph, "wif", [128, 8, 8], BF16)
                t_wif = trk("wif")
                POOL.dma(wif[:], w_in_r[:, :, OFF_IF:OFF_IF + 8], t_wif, w=[t_wif])
                gbias = sb(ph, "gbias", [128, 8], F32)
                hgb = sb(ph, "hgb", [128, 512], F32)
                cwT = sb(ph, "cwT", [128, 4, 8], F32)
                cbT = sb(ph, "cbT", [128, 8], F32)
                t_mc = trk("mlconst")
                SP.dma(gbias[:], ml_gate_b.partition_broadcast(128), t_mc, w=[t_mc])
                SP.dma(hgb[:], ml_head_g.partition_broadcast(128), t_mc, w=[t_mc])
                cw_sb = sb(ph, "cw_sb", [5, 1024], F32)
                SP.dma(cw_sb[0:4, :], ml_conv_w, t_mc, w=[t_mc])
                SP.dma(cw_sb[4:5, :], ml_conv_b, t_mc, w=[t_mc])
                p, tp = nf()
                PE.group([("transpose", (p[:, j * 5:(j + 1) * 5], cw_sb[:, j * 128:(j + 1) * 128], idf[0:5, 0:5]), {})
                          for j in range(8)], r=[t_mc, t_c], w=[tp])
                p3 = p[:, 0:40].rearrange("p (j w) -> p j w", w=5)
                DVE("tensor_copy", cwT[:], p3[:, :, 0:4].rearrange("p j w -> p w j"), r=[tp], w=[t_mc])
                DVE("tensor_copy", cbT[:], p3[:, :, 4], r=[tp], w=[t_mc])
                stop(3.1)
                ifz = sb(ph, "ifz", [128, NT, 8], F32)
                t_ifz = trk("ifz")
                logf = sb(ph, "logf", [128, NT, 4], F32)
                t_lf = trk("logf")
                ib = sb(ph, "ib", [128, NT, 4], F32)
                t_ib = trk("ib")
                for tt in range(NT):
                    tsl = slice(tt * 128, (tt + 1) * 128)
                    p, tp = nf()
                    mm8(p[:, 0:8], lambda kc: hT[:, kc, tsl], lambda kc: wif[:, kc, :], 8, [t_wif, t_hT[tt]], [tp])
                    DVE("tensor_tensor", ifz[:, tt, :], p[:, 0:8], gbias[:], ALU.add, r=[tp, t_mc], w=[t_ifz])
                ACT("activation", logf[:], ifz[:, :, 4:8], AF.Exp, scale=-1.0, r=[t_ifz], w=[t_lf])
                DVE("tensor_scalar_add", logf[:], logf[:], 1.0, r=[t_lf], w=[t_lf])
                ACT("activation", logf[:], logf[:], AF.Ln, r=[t_lf], w=[t_lf])
                DVE("tensor_scalar", logf[:], logf[:], -1.0, None, ALU.mult, r=[t_lf], w=[t_lf])
                lf_hi = sb(ph, "lf_hi", [128, NT, 4], BF16)
                lf_lo = sb(ph, "lf_lo", [128, NT, 4], BF16)
                lf_t = sb(ph, "lf_t", [128, NT, 4], F32)
                DVE("tensor_copy", lf_hi[:], logf[:], r=[t_lf], w=[t_lf])
                DVE("tensor_copy", lf_t[:], lf_hi[:], r=[t_lf], w=[t_lf])
                DVE("tensor_tensor", lf_t[:], logf[:], lf_t[:], ALU.subtract, r=[t_lf], w=[t_lf])
                DVE("tensor_copy", lf_lo[:], lf_t[:], r=[t_lf], w=[t_lf])
                for tt in range(NT):
                    p, tp = nf()
                    PE.group([("matmul", (p[:, 0:4], tri_b[:], lf_hi[:, tt, :]), dict(start=True, stop=False)),
                              ("matmul", (p[:, 0:4], tri_b[:], lf_lo[:, tt, :]), dict(start=False, stop=True))],
                             r=[t_c, t_lf], w=[tp])
                    DVE("tensor_tensor", ib[:, tt, :], ifz[:, tt, 0:4], p[:, 0:4], ALU.subtract, r=[tp, t_ifz], w=[t_ib])

                stop(3.2)
                wml = [sb(ph, f"wml{i}", [128, 8, 512], BF16) for i in range(2)]
                t_wml = [trk(f"wml{i}") for i in range(2)]
                pre = sb(ph, "pre", [128, S + 3], F32)
                t_pre = trk("pre")
                cacc = sb(ph, "cacc", [128, S], F32)
                t_cacc = trk("cacc")
                qm = sb(ph, "qm", [128, S], BF16)
                km = sb(ph, "km", [128, S], BF16)
                t_qm = trk("qm")
                t_km = trk("km")
                ktok = sb(ph, "ktok", [128, NT, 128], BF16)
                t_kt = trk("ktok")
                va = sb(ph, "va", [128, NT, 129], BF16)
                t_va = trk("va")
                vT = sb(ph, "vT", [128, S], BF16)
                t_vT = trk("vT")
                ogT = sb(ph, "ogT", [128, S], BF16)
                t_ogT = trk("ogT")
                Ct = sb(ph, "Ct", [128, 129], F32)
                Ctb = sb(ph, "Ctb", [128, 129], BF16)
                t_Ct = trk("Ct")
                t_Ctb = trk("Ctb")
                lfb = [sb(ph, f"lfb{i}", [128, 2, 128], BF16) for i in range(2)]
                t_lfb = [trk(f"lfb{i}") for i in range(2)]
                arg = [sb(ph, f"arg{i}", [128, 128], F32) for i in range(2)]
                DT = [sb(ph, f"DT{i}", [128, 128], F32) for i in range(2)]
                t_DT = [trk(f"DT{i}") for i in range(2)]
                eb = [sb(ph, f"eb{i}", [128, 128], F32) for i in range(2)]
                t_eb = [trk(f"eb{i}") for i in range(2)]
                sm = [sb(ph, f"sm{i}", [128, 8], F32) for i in range(2)]
                t_sm = [trk(f"sm{i}") for i in range(2)]
                sqk = [sb(ph, f"sqk{i}", [128, 128], BF16) for i in range(2)]
                t_sqk = [trk(f"sqk{i}") for i in range(2)]
                qsc = [sb(ph, f"qsc{i}", [128, 128], BF16) for i in range(2)]
                t_qsc = [trk(f"qsc{i}") for i in range(2)]
                vwt = [sb(ph, f"vwt{i}", [128, 129], BF16) for i in range(2)]
                t_vwt = [trk(f"vwt{i}") for i in range(2)]
                hnm = sb(ph, "hnm", [128, 128], F32)
                hsq = sb(ph, "hsq", [128, 128], F32)
                hst = sb(ph, "hst", [128, 4], F32)
                t_h = trk("hwork")
                yml = sb(ph, "yml", [128, 128], BF16)
                t_ymlb = trk("ymlb")
                DVE("memset", pre[:, 0:3], 0.0, w=[t_pre])
                DVE("memset", va[:, :, 128:129], 1.0, w=[t_va])
                for hd in range(4):
                    wi = hd % 2
                    for pi, c0 in enumerate((OFF_ML + hd * 128, OFF_ML + 512 + hd * 128, OFF_ML + 1024 + hd * 128,
                                             OFF_O + hd * 128)):
                        POOL.dma(wml[wi][:, :, pi * 128:(pi + 1) * 128], w_in_r[:, :, c0:c0 + 128], t_wml[wi], w=[t_wml[wi]])
                    for qk in range(2):
                        j = qk * 4 + hd
                        for tb in range(4):
                            ts_ = slice(tb * 512, (tb + 1) * 512)
                            p, tp = nf()
                            mm8(p[:, :], lambda kc: wml[wi][:, kc, qk * 128:(qk + 1) * 128], lambda kc: hT[:, kc, ts_], 8,
                                [t_wml[wi]] + t_hT[4 * tb:4 * tb + 4], [tp])
                            evac(pre[:, 3 + tb * 512:3 + (tb + 1) * 512], p[:, :], [tp], [t_pre])
                        DVE("tensor_scalar", cacc[:], pre[:, 3:3 + S], cwT[:, 3, j:j + 1], None, ALU.mult,
                            r=[t_pre, t_mc], w=[t_cacc])
                        for wv in range(3):
                            DVE("scalar_tensor_tensor", cacc[:], pre[:, wv:wv + S], cwT[:, wv, j:j + 1], cacc[:],
                                ALU.mult, ALU.add, r=[t_pre, t_mc], w=[t_cacc])
                        dst, tdst = (qm, t_qm) if qk == 0 else (km, t_km)
                        ACT("activation", dst[:], cacc[:], AF.Silu, bias=cbT[:, j:j + 1], r=[t_cacc, t_mc], w=[tdst])
                    stop(3.3)
                    for which in (2, 3):
                        for tb in range(4):
                            ts_ = slice(tb * 512, (tb + 1) * 512)
                            p, tp = nf()
                            mm8(p[:, :], lambda kc: wml[wi][:, kc, which * 128:(which + 1) * 128], lambda kc: hT[:, kc, ts_], 8,
                                [t_wml[wi]] + t_hT[4 * tb:4 * tb + 4], [tp])
                            if which == 2:
                                evac(vT[:, ts_], p[:, :], [tp], [t_vT])
                            else:
                                ACT("activation", ogT[:, ts_], p[:, :], AF.Sigmoid, r=[tp], w=[t_ogT])
                    for tt in range(NT):
                        tsl = slice(tt * 128, (tt + 1) * 128)
                        pb_, tpb = nb()
                        PE("transpose", pb_[:, 0:128], km[:, tsl], idb[:], r=[t_km, t_c], w=[tpb])
                        evac(ktok[:, tt, :], pb_[:, 0:128], [tpb], [t_kt], eng=0)
                        pb_, tpb = nb()
                        PE("transpose", pb_[:, 0:128], vT[:, tsl], idb[:], r=[t_vT, t_c], w=[tpb])
                        evac(va[:, tt, 0:128], pb_[:, 0:128], [tpb], [t_va], eng=0)
                    stop(3.4)
                    DVE("memset", Ct[:], 0.0, w=[t_Ct])
                    DVE("memset", Ctb[:], 0.0, w=[t_Ctb])
                    def ml_pre(tt):
                        b = tt % 2
                        tsl = slice(tt * 128, (tt + 1) * 128)
                        DVE("tensor_copy", lfb[b][:, 0, :], lf_hi[:, tt, hd:hd + 1].to_broadcast([128, 128]), r=[t_lf], w=[t_lfb[b]])
                        DVE("tensor_copy", lfb[b][:, 1, :], lf_lo[:, tt, hd:hd + 1].to_broadcast([128, 128]), r=[t_lf], w=[t_lfb[b]])
                        pbr, tbr = nf()
                        PE.group([("matmul", (pbr[:, 0:128], lfb[b][:, 0, :], tri_b[:]), dict(start=True, stop=False)),
                                  ("matmul", (pbr[:, 0:128], lfb[b][:, 1, :], tri_b[:]), dict(start=False, stop=True))],
                                 r=[t_lfb[b], t_c], w=[tbr])
                        DVE("tensor_tensor", arg[b][:], pbr[:, 0:128], negm[:], ALU.add, r=[tbr, t_c], w=[t_DT[b]])
                        ACT("activation", DT[b][:], arg[b][:], AF.Exp, bias=ib[:, tt, hd:hd + 1], r=[t_DT[b], t_ib], w=[t_DT[b]])
                        ACT("activation", eb[b][:], pbr[:, 0:128], AF.Exp, r=[tbr], w=[t_eb[b]])
                        DVE("tensor_copy", sm[b][:, 0:1], pbr[:, 127:128], r=[tbr], w=[t_sm[b]])
                        ACT("activation", sm[b][:, 1:2], ib[:, tt, hd:hd + 1], AF.Exp, bias=sm[b][:, 0:1], r=[t_sm[b], t_ib], w=[t_sm[b]])
                        ACT("activation", sm[b][:, 2:3], sm[b][:, 0:1], AF.Exp, r=[t_sm[b]], w=[t_sm[b]])
                        pS, tS = nf()
                        PE("matmul", pS[:, 0:128], km[:, tsl], qm[:, tsl], start=True, stop=True, r=[t_km, t_qm], w=[tS])
                        DVE("scalar_tensor_tensor", sqk[b][:], pS[:, 0:128], 128.0 ** -0.5, DT[b][:], ALU.mult, ALU.mult,
                            r=[tS, t_DT[b]], w=[t_sqk[b]])
                        DVE("tensor_tensor", qsc[b][:], qm[:, tsl], eb[b][:], ALU.mult, r=[t_qm, t_eb[b]], w=[t_qsc[b]])
                        DVE("tensor_scalar", vwt[b][:], va[:, tt, :], sm[b][:, 1:2], 128.0 ** -0.5, ALU.mult, ALU.mult,
                            r=[t_va, t_sm[b]], w=[t_vwt[b]])

                    def ml_seq(tt):
                        b = tt % 2
                        tsl = slice(tt * 128, (tt + 1) * 128)
                        pN, tN = nf()
                        PE.group([("matmul", (pN[:, 0:129], sqk[b][:], va[:, tt, :]), dict(start=True, stop=False)),
                                  ("matmul", (pN[:, 0:129], qsc[b][:], Ctb[:]), dict(start=False, stop=True))],
                                 r=[t_sqk[b], t_va, t_qsc[b], t_Ctb], w=[tN])
                        pU, tU = nf()
                        PE("matmul", pU[:, 0:129], ktok[:, tt, :], vwt[b][:], start=True, stop=True, r=[t_kt, t_vwt[b]], w=[tU])
                        DVE("scalar_tensor_tensor", Ct[:], Ct[:], sm[b][:, 2:3], pU[:, 0:129], ALU.mult, ALU.add,
                            r=[tU, t_sm[b]], w=[t_Ct])
                        DVE("tensor_copy", Ctb[:], Ct[:], r=[t_Ct], w=[t_Ctb])
                        DVE("tensor_scalar", hst[:, 0:1], pN[:, 128:129], -1.0, None, ALU.mult, r=[tN], w=[t_h])
                        DVE("tensor_tensor", hst[:, 0:1], hst[:, 0:1], pN[:, 128:129], ALU.max, r=[tN], w=[t_h])
                        DVE("tensor_scalar_max", hst[:, 0:1], hst[:, 0:1], 1.0, w=[t_h])
                        DVE("reciprocal", hst[:, 0:1], hst[:, 0:1], w=[t_h])
                        DVE("tensor_scalar", hnm[:], pN[:, 0:128], hst[:, 0:1], None, ALU.mult, r=[tN], w=[t_h])
                        DVE("tensor_tensor", hsq[:], hnm[:], hnm[:], ALU.mult, w=[t_h])
                        DVE("tensor_reduce", hst[:, 1:2], hsq[:], AX.X, ALU.add, w=[t_h])
                        DVE("tensor_scalar", hst[:, 1:2], hst[:, 1:2], 1.0 / 128, EPS, ALU.mult, ALU.add, w=[t_h])
                        ACT("activation", hst[:, 2:3], hst[:, 1:2], AF.Ln, r=[t_h], w=[t_h])
                        ACT("activation", hst[:, 3:4], hst[:, 2:3], AF.Exp, scale=-0.5, r=[t_h], w=[t_h])
                        DVE("scalar_tensor_tensor", yml[:], hnm[:], hst[:, 3:4], hgb[:, hd * 128:(hd + 1) * 128],
                            ALU.mult, ALU.mult, r=[t_h, t_mc], w=[t_ymlb])
                        pb_, tpb = nb()
                        PE("transpose", pb_[:, 0:128], yml[:], idb[:], r=[t_ymlb, t_c], w=[tpb])
                        DVE("tensor_tensor", y_mlT[:, hd, tsl], pb_[:, 0:128], ogT[:, tsl], ALU.mult, r=[tpb, t_ogT], w=[t_yml])

                    ml_pre(0)
                    for tt in range(NT):
                        if tt + 1 < NT:
                            ml_pre(tt + 1)
                        ml_seq(tt)
                        if tt == 1:
                            stop(3.5)
                K.barrier()
            if debug:
                POOL.dma(dbg["d_ymlT"], y_mlT[:].rearrange("p a b -> p (a b)"), trk("dbg2"), r=[t_yml])
            stop(3)

            y_memT = sb(sB, "y_memT", [128, 4, S], BF16)
            t_ymem = trk("ymemT")
            with ExitStack() as ph:
                gb = sb(ph, "gb4", [128, D], F32)
                t_gb = trk("gb4")
                SP.dma(gb[:], g_mem.partition_broadcast(128), t_gb, w=[t_gb])
                wmkv = sb(ph, "wmkv", [128, 8, 1024], BF16)
                t_wm = trk("wmkv")
                wmr = w_mem_kv.rearrange("(kc p) n -> p kc n", p=128)
                for c in range(2):
                    POOL.dma(wmkv[:, :, c * 512:(c + 1) * 512], wmr[:, :, c * 512:(c + 1) * 512], t_wm, w=[t_wm])
                wqm = sb(ph, "wqm", [128, 8, 512], BF16)
                POOL.dma(wqm[:], w_in_r[:, :, OFF_QM:OFF_QM + 512], t_wm, w=[t_wm])
                mt_ = [sb(ph, f"mt{i}", [128, D], F32) for i in range(2)]
                sq = sb(ph, "sq4", [128, D], F32)
                mn = [sb(ph, f"mn{i}", [128, D], BF16) for i in range(2)]
                st = [sb(ph, f"st4{i}", [128, 2], F32) for i in range(2)]
                t_m = [trk(f"mem{i}") for i in range(2)]
                memT = sb(ph, "memT", [128, 8, 256], BF16)
                t_memT = trk("memT")
                for i in range(2):
                    SP.dma(mt_[i][:], mem[i * 128:(i + 1) * 128, :], t_m[i], w=[t_m[i]])
                    ACT("activation", sq[:], mt_[i][:], AF.Square, accum_out=st[i][:, 0:1], r=[t_m[i]], w=[t_m[i]])
                    rms_rstd(st[i], t_m[i], D)
                    DVE("scalar_tensor_tensor", mn[i][:], mt_[i][:], st[i][:, 1:2], gb[:], ALU.mult, ALU.mult,
                        r=[t_m[i], t_gb], w=[t_m[i]])
                    to_T(mn[i], t_m[i], lambda i=i: memT[:, :, i * 128:(i + 1) * 128], t_memT, 8)
                KmT = sb(ph, "KmT", [128, 4, 256], BF16)
                Vma = sb(ph, "Vma", [128, 2, 4, 129], BF16)
                t_kv = trk("memkv")
                DVE("memset", Vma[:, :, :, 128:129], 1.0, w=[t_kv])
                for h in range(4):
                    p, tp = nf()
                    mm8(p[:, 0:256], lambda kc: wmkv[:, kc, h * 128:(h + 1) * 128], lambda kc: memT[:, kc, :], 8,
                        [t_wm, t_memT], [tp])
                    evac(KmT[:, h, :], p[:, 0:256], [tp], [t_kv])
                for m2 in range(2):
                    p, tp = nf()
                    mm8(p[:, :], lambda kc: memT[:, kc, m2 * 128:(m2 + 1) * 128], lambda kc: wmkv[:, kc, 512:1024], 8,
                        [t_wm, t_memT], [tp])
                    evac(Vma[:, m2, :, 0:128], p[:].rearrange("p (a b) -> p a b", a=4), [tp], [t_kv])
                qmT = sb(ph, "qmT", [128, S], BF16)
                t_qmT = trk("qmT")
                Pm = [sb(ph, f"Pm{i}", [128, 512], BF16) for i in range(4)]
                t_Pm = [trk(f"Pm{i}") for i in range(4)]
                rcm_ = [sb(ph, f"rcm{i}", [128, 1], F32) for i in range(2)]
                ymb_ = [sb(ph, f"ymb{i}", [128, 128], BF16) for i in range(2)]
                t_ymb_ = [trk(f"ymb{i}") for i in range(2)]
                for h in range(4):
                    for tb in range(4):
                        ts_ = slice(tb * 512, (tb + 1) * 512)
                        p, tp = nf()
                        mm8(p[:, :], lambda kc: wqm[:, kc, h * 128:(h + 1) * 128], lambda kc: hT[:, kc, ts_], 8,
                            [t_wm] + t_hT[4 * tb:4 * tb + 4], [tp])
                        evac(qmT[:, ts_], p[:, :], [tp], [t_qmT], scale=128.0 ** -0.5)
                    for tb in range(4):
                        ts_ = slice(tb * 512, (tb + 1) * 512)
                        pb2 = (tb % 2) * 2
                        for m2 in range(2):
                            pS, tS = nf()
                            PE("matmul", pS[:, :], KmT[:, h, m2 * 128:(m2 + 1) * 128], qmT[:, ts_], start=True, stop=True,
                               r=[t_kv, t_qmT], w=[tS])
                            ACT("activation", Pm[pb2 + m2][:], pS[:, :], AF.Exp, r=[tS], w=[t_Pm[pb2 + m2]])
                        for q in range(4):
                            tt = tb * 4 + q
                            yb = q % 2
                            rcm, ymb, t_ymb = rcm_[yb], ymb_[yb], t_ymb_[yb]
                            pO, tO = nf()
                            PE.group([("matmul", (pO[:, 0:129], Pm[pb2 + m2][:, q * 128:(q + 1) * 128], Vma[:, m2, h, :]),
                                       dict(start=(m2 == 0), stop=(m2 == 1))) for m2 in range(2)],
                                     r=[t_Pm[pb2], t_Pm[pb2 + 1], t_kv], w=[tO])
                            DVE("reciprocal", rcm[:], pO[:, 128:129], r=[tO], w=[t_ymb])
                            DVE("tensor_scalar", ymb[:], pO[:, 0:128], rcm[:, 0:1], None, ALU.mult, r=[tO], w=[t_ymb])
                            to_T(ymb, t_ymb, lambda h=h, tt=tt: y_memT[:, h:h + 1, tt * 128:(tt + 1) * 128], t_ymem, 1)
                K.barrier()
            if debug:
                POOL.dma(dbg["d_ymemT"], y_memT[:].rearrange("p a b -> p (a b)"), trk("dbg3"), r=[t_ymem])
            stop(4)

            with ExitStack() as ph:
                wg5 = [sb(ph, f"wg5{i}", [128, 8, 3, 128], BF16) for i in range(2)]
                wp5 = [sb(ph, f"wp5{i}", [128, 16, 128], BF16) for i in range(2)]
                t_w5 = [trk(f"w5{i}") for i in range(2)]
                sgm = [sb(ph, f"sgm{i}", [128, 512], F32) for i in range(3)]
                t_sgm = [trk(f"sgm{i}") for i in range(3)]
                acc = sb(ph, "acc5", [128, 512], F32)
                tmp = sb(ph, "tmp5", [128, 512], F32)
                t_acc = trk("acc5")
                wpn_r = w_proj_nsa.rearrange("(kc p) n -> p kc n", p=128)
                wpl_r = w_proj_ml.rearrange("(kc p) n -> p kc n", p=128)
                wpm_r = w_proj_mem.rearrange("(kc p) n -> p kc n", p=128)
                for c in range(8):
                    i = c % 2
                    cs = slice(c * 128, (c + 1) * 128)
                    for b in range(3):
                        POOL.dma(wg5[i][:, :, b, :], w_in_r[:, :, OFF_GM + b * 1024 + c * 128:OFF_GM + b * 1024 + (c + 1) * 128],
                                 t_w5[i], w=[t_w5[i]])
                    POOL.dma(wp5[i][:, 0:8, :], wpn_r[:, :, cs], t_w5[i], w=[t_w5[i]])
                    POOL.dma(wp5[i][:, 8:12, :], wpl_r[:, :, cs], t_w5[i], w=[t_w5[i]])
                    POOL.dma(wp5[i][:, 12:16, :], wpm_r[:, :, cs], t_w5[i], w=[t_w5[i]])
                    for tb in range(4):
                        ts_ = slice(tb * 512, (tb + 1) * 512)
                        rh = [t_w5[i]] + t_hT[4 * tb:4 * tb + 4]
                        for b in range(3):
                            p, tp = nf()
                            mm8(p[:, :], lambda kc: wg5[i][:, kc, b, :], lambda kc: hT[:, kc, ts_], 8, rh, [tp])
                            ACT("activation", sgm[b][:], p[:, :], AF.Sigmoid, r=[tp], w=[t_sgm[b]])
                        p0, tp0 = nf()
                        mm8(p0[:, :], lambda kc: wp5[i][:, kc, :], lambda kc: y_nsaT[:, kc, ts_], 8, [t_w5[i], t_ynsa], [tp0])
                        p1, tp1 = nf()
                        mm8(p1[:, :], lambda kc: wp5[i][:, 8 + kc, :], lambda kc: y_mlT[:, kc, ts_], 4, [t_w5[i], t_yml], [tp1])
                        p2, tp2 = nf()
                        mm8(p2[:, :], lambda kc: wp5[i][:, 12 + kc, :], lambda kc: y_memT[:, kc, ts_], 4, [t_w5[i], t_ymem], [tp2])
                        DVE("tensor_tensor", acc[:], p0[:, :], sgm[0][:], ALU.mult, r=[tp0, t_sgm[0]], w=[t_acc])
                        DVE("tensor_tensor", tmp[:], p1[:, :], sgm[1][:], ALU.mult, r=[tp1, t_sgm[1]], w=[t_acc])
                        DVE("tensor_tensor", acc[:], acc[:], tmp[:], ALU.add, w=[t_acc])
                        DVE("tensor_tensor", tmp[:], p2[:, :], sgm[2][:], ALU.mult, r=[tp2, t_sgm[2]], w=[t_acc])
                        DVE("tensor_tensor", yT[:, c, ts_], acc[:], tmp[:], ALU.add, r=[t_acc], w=[t_yT])
                K.barrier()
        if debug:
            POOL.dma(dbg["d_yT"], yT[:].rearrange("p a b -> p (a b)"), trk("dbg4"), r=[t_yT])
            K.barrier()
        stop(5)

        h2T = sb(es, "h2T", [128, 8, S], BF16)
        t_h2T = [trk(f"h2T{i}") for i in range(NT)]
        t_out = [trk(f"out{i}") for i in range(NT)]
        with ExitStack() as ph:
            wo = sb(ph, "wo", [128, 8, 1024], BF16)
            t_wo = trk("wo")
            wor = w_out.rearrange("(kc p) n -> p kc n", p=128)
            for c in range(2):
                POOL.dma(wo[:, :, c * 512:(c + 1) * 512], wor[:, :, c * 512:(c + 1) * 512], t_wo, w=[t_wo])
            gb = sb(ph, "gb6", [128, D], F32)
            gb2 = sb(ph, "gb6b", [128, D], F32)
            t_gb = trk("gb6")
            SP.dma(gb[:], g_post_mix.partition_broadcast(128), t_gb, w=[t_gb])
            SP.dma(gb2[:], g_pre_ffn.partition_broadcast(128), t_gb, w=[t_gb])
            xt = [sb(ph, f"xt6{i}", [128, D], F32) for i in range(2)]
            t_xt = [trk(f"xt6{i}") for i in range(2)]
            x1 = [sb(ph, f"x16{i}", [128, D], F32) for i in range(2)]
            t_x1 = [trk(f"x16{i}") for i in range(2)]
            sq = sb(ph, "sq6", [128, D], F32)
            t_sq = trk("sq6")
            tm6 = sb(ph, "tm6", [128, D], F32)
            t_tm = trk("tm6")
            hn = [sb(ph, f"hn6{i}", [128, D], BF16) for i in range(2)]
            t_hn = [trk(f"hn6{i}") for i in range(2)]
            st = [sb(ph, f"st6{i}", [128, 4], F32) for i in range(2)]
            t_st = [trk(f"st6{i}") for i in range(2)]
            SP.dma(xt[0][:], x[0:128, :], t_xt[0], w=[t_xt[0]])
            for tt in range(NT):
                i = tt % 2
                tsl = slice(tt * 128, (tt + 1) * 128)
                if tt + 1 < NT:
                    SP.dma(xt[1 - i][:], x[(tt + 1) * 128:(tt + 2) * 128, :], t_xt[1 - i], w=[t_xt[1 - i]])
                pu = []
                for hf in range(2):
                    p, tp = nf()
                    mm8(p[:, :], lambda kc: yT[:, kc, tsl], lambda kc: wo[:, kc, hf * 512:(hf + 1) * 512], 8, [t_yT, t_wo], [tp])
                    ACT("activation", sq[:, hf * 512:(hf + 1) * 512], p[:, :], AF.Square, accum_out=st[i][:, 2 + hf:3 + hf],
                        r=[tp], w=[t_sq, t_st[i]])
                    pu.append((p, tp))
                DVE("tensor_tensor", st[i][:, 0:1], st[i][:, 2:3], st[i][:, 3:4], ALU.add, r=[t_st[i]], w=[t_st[i]])
                rms_rstd(st[i], t_st[i], D)
                for hf in range(2):
                    p, tp = pu[hf]
                    hs = slice(hf * 512, (hf + 1) * 512)
                    DVE("scalar_tensor_tensor", tm6[:, hs], p[:, :], st[i][:, 1:2], gb[:, hs], ALU.mult, ALU.mult,
                        r=[tp, t_st[i], t_gb], w=[t_tm])
                DVE("tensor_tensor", x1[i][:], tm6[:], xt[i][:], ALU.add, r=[t_tm, t_xt[i]], w=[t_x1[i]])
                SP.dma(out[tsl, :], x1[i][:], t_x1[i], r=[t_x1[i]], w=[t_out[tt]])
                if debug:
                    SP.dma(dbg["d_x1"][tsl, :], x1[i][:], t_x1[i], r=[t_x1[i]])
                ACT("activation", sq[:], x1[i][:], AF.Square, accum_out=st[i][:, 0:1], r=[t_x1[i]], w=[t_sq, t_st[i]])
                rms_rstd(st[i], t_st[i], D)
                DVE("scalar_tensor_tensor", hn[i][:], x1[i][:], st[i][:, 1:2], gb2[:], ALU.mult, ALU.mult,
                    r=[t_x1[i], t_st[i], t_gb], w=[t_hn[i]])
                to_T(hn[i], t_hn[i], lambda tsl=tsl: h2T[:, :, tsl], t_h2T[tt], 8)
            K.barrier()

        with ExitStack() as ph:
            wd = sb(ph, "wd", [128, NF, 1024], BF16)
            t_wd = trk("wd")
            wdr = w_ffn_down.rearrange("(f p) n -> p f n", p=128)
            wfr = w_ffn_in.rearrange("(kc p) n -> p kc n", p=128)
            wf = [sb(ph, f"wf{i}", [128, 8, 2, 256], BF16) for i in range(2)]
            t_wf = [trk(f"wf{i}") for i in range(2)]
            aT = sb(ph, "aT", [128, NF, 1024], BF16)
            t_aT = trk("aT")
            sl = [sb(ph, f"sl{i}", [128, 512], F32) for i in range(2)]
            t_sl = [trk(f"sl{i}") for i in range(2)]
            gb = sb(ph, "gb7", [128, D], F32)
            t_gb = trk("gb7")
            SP.dma(gb[:], g_post_ffn.partition_broadcast(128), t_gb, w=[t_gb])
            xt = [sb(ph, f"xt7{i}", [128, D], F32) for i in range(2)]
            t_xt = [trk(f"xt7{i}") for i in range(2)]
            sq = sb(ph, "sq7", [128, D], F32)
            t_sq = trk("sq7")
            tm7 = sb(ph, "tm7", [128, D], F32)
            t_tm = trk("tm7")
            fo = [sb(ph, f"fo{i}", [128, D], F32) for i in range(2)]
            t_fo = [trk(f"fo{i}") for i in range(2)]
            st = [sb(ph, f"st7{i}", [128, 4], F32) for i in range(2)]
            t_st = [trk(f"st7{i}") for i in range(2)]
            t_fin = trk("final")
            nw = 0
            nsl = 0
            for hf in range(2):
                for f in range(NF):
                    f2 = f % 2
                    if f2 == 0:
                        i = nw % 2
                        nw += 1
                        POOL.dma(wf[i][:, :, 0, :], wfr[:, :, f * 128:(f + 2) * 128], t_wf[i], w=[t_wf[i]])
                        POOL.dma(wf[i][:, :, 1, :], wfr[:, :, DFF + f * 128:DFF + (f + 2) * 128], t_wf[i], w=[t_wf[i]])
                    if hf == 0:
                        POOL.dma(wd[:, f, :], wdr[:, f, :], t_wd, w=[t_wd])
                    for tbl in range(2):
                        tb = hf * 2 + tbl
                        ts_ = slice(tb * 512, (tb + 1) * 512)
                        rh = [t_wf[i]] + t_h2T[4 * tb:4 * tb + 4]
                        pg, tpg = nf()
                        mm8(pg[:, :], lambda kc: wf[i][:, kc, 0, f2 * 128:(f2 + 1) * 128], lambda kc: h2T[:, kc, ts_], 8, rh, [tpg])
                        pu_, tpu = nf()
                        mm8(pu_[:, :], lambda kc: wf[i][:, kc, 1, f2 * 128:(f2 + 1) * 128], lambda kc: h2T[:, kc, ts_], 8, rh, [tpu])
                        j = nsl % 2
                        nsl += 1
                        ACT("activation", sl[j][:], pg[:, :], AF.Silu, r=[tpg], w=[t_sl[j]])
                        DVE("tensor_tensor", aT[:, f, tbl * 512:(tbl + 1) * 512], sl[j][:], pu_[:, :], ALU.mult,
                            r=[t_sl[j], tpu], w=[t_aT])
                t0_ = hf * 8
                SP.dma(xt[t0_ % 2][:], out[t0_ * 128:(t0_ + 1) * 128, :], t_xt[t0_ % 2], r=[t_out[t0_]], w=[t_xt[t0_ % 2]])
                for ttl in range(8):
                    tt = hf * 8 + ttl
                    i = tt % 2
                    tsl = slice(tt * 128, (tt + 1) * 128)
                    if ttl + 1 < 8:
                        SP.dma(xt[1 - i][:], out[(tt + 1) * 128:(tt + 2) * 128, :], t_xt[1 - i], r=[t_out[tt + 1]], w=[t_xt[1 - i]])
                    pu = []
                    for h2 in range(2):
                        p, tp = nf()
                        mm8(p[:, :], lambda f: aT[:, f, ttl * 128:(ttl + 1) * 128], lambda f: wd[:, f, h2 * 512:(h2 + 1) * 512],
                            NF, [t_aT, t_wd], [tp])
                        ACT("activation", sq[:, h2 * 512:(h2 + 1) * 512], p[:, :], AF.Square, accum_out=st[i][:, 2 + h2:3 + h2],
                            r=[tp], w=[t_sq, t_st[i]])
                        pu.append((p, tp))
                    DVE("tensor_tensor", st[i][:, 0:1], st[i][:, 2:3], st[i][:, 3:4], ALU.add, r=[t_st[i]], w=[t_st[i]])
                    rms_rstd(st[i], t_st[i], D)
                    for h2 in range(2):
                        p, tp = pu[h2]
                        hs = slice(h2 * 512, (h2 + 1) * 512)
                        DVE("scalar_tensor_tensor", tm7[:, hs], p[:, :], st[i][:, 1:2], gb[:, hs], ALU.mult, ALU.mult,
                            r=[tp, t_st[i], t_gb], w=[t_tm])
                    DVE("tensor_tensor", fo[i][:], tm7[:], xt[i][:], ALU.add, r=[t_tm, t_xt[i]], w=[t_fo[i]])
                    SP.dma(out[tsl, :], fo[i][:], t_fo[i], r=[t_fo[i]], w=[t_out[tt], t_fin])
            K.barrier()
    except _Stop:
        pass
    return nc


def _consts():
    c = {}
    c["c_ident"] = np.eye(128, dtype=np.float32)
    s = np.arange(128)
    tri = (s[:, None] <= s[None, :]).astype(np.float32)
    c["c_tri"] = tri
    c["c_trin"] = (1.0 - tri).astype(np.float32)
    c["c_negm"] = ((1.0 - tri) * -1e4).astype(np.float32)
    n = np.arange(128)
    t = np.arange(S)
    cm = ((n[:, None] * 16 + 31) <= t[None, :]).astype(np.float32)
    cm[127, :] = 0
    c["c_cmpmask"] = cm
    c0 = np.arange(128) * 16
    s0 = np.arange(32) * 64
    ov = np.minimum(c0[:, None] + 32, s0[None, :] + 64) - np.maximum(c0[:, None], s0[None, :])
    cmap = np.clip(ov, 0, None).astype(np.float32) / 32
    cmap[127, :] = 0
    c["c_cmap"] = cmap
    selm = np.zeros((128, 3, 16, 32), np.float32)
    for qt in range(16):
        for q in range(128):
            qb = (qt * 128 + q) // 64
            j = np.arange(32)
            rel = qb - j
            causal = rel >= 0
            forced = causal & ((j == 0) | (rel < 2))
            selm[q, 0, qt] = (causal & ~forced)
            selm[q, 1, qt] = np.where(forced, BIG, np.where(causal, 0.0, -BIG))
            selm[q, 2, qt] = causal
    c["c_selm"] = selm.reshape(128, -1)
    E = np.zeros((32, 16, 128), np.float32)
    for kt in range(16):
        for k in range(128):
            E[2 * kt + k // 64, kt, k] = 1.0
    c["c_E"] = E.reshape(32, -1)
    return c


_NC = {}


def _get_nc(debug=False):
    return build(debug)


def make_in_maps(inputs, cores):
    cst = _consts()
    maps = []
    for b in cores:
        m = dict(cst)
        for k, v in inputs.items():
            v = np.asarray(v, dtype=np.float32)
            if k in ("x", "mem"):
                m[k] = np.ascontiguousarray(v[b])
            else:
                a = v[0]
                if a.ndim == 1:
                    a = a[None, :]
                m[k] = np.ascontiguousarray(a)
        maps.append(m)
    return maps


def kernel(**inputs):
    nc = _get_nc(False)
    maps = make_in_maps(inputs, list(range(8)))
    res = run_bass_kernel_spmd(nc, maps, core_ids=list(range(8)))
    return np.stack([np.asarray(r["out"], dtype=np.float32) for r in res.results], axis=0)
```
